# Optimizing a Trainium2 kernel written in Bass

```python
import math
import jax, jax.numpy as jnp
from jax import lax
import numpy as np

D_MODEL = 1024
BATCH = 4
SEQ = 8192
DEPTH = 1
DEC_BATCH = 32
DEC_SEQ = 32
PAST_LEN = 2048

CHUNK = 64
N_HEADS = 8
HEAD_DIM = 64
SB_WIDTH = N_HEADS * HEAD_DIM
CONV_WIDTH = 512
CONV_K = 3
PLE_DIM = 256
D_FF = 4 * D_MODEL
Q_BLOCK = 128
EPS = 1e-6
SPLITS = tuple(int(s) for s in np.cumsum([SB_WIDTH, SB_WIDTH, SB_WIDTH, CONV_WIDTH, CONV_WIDTH, CONV_WIDTH, D_MODEL]))
N_PROJ = 3 * SB_WIDTH + 3 * CONV_WIDTH + 2 * D_MODEL

kernel_name = "stick_breaking_shortconv_griffin_step"


def rmsnorm(x, g):
    xf = x.astype(jnp.float32)
    y = xf * lax.rsqrt(jnp.mean(xf * xf, axis=-1, keepdims=True) + EPS)
    return y.astype(x.dtype) * g


def stick_breaking(q, k, v, q_pos, k_pos):
    scale = 1.0 / math.sqrt(HEAD_DIM)
    z = jnp.einsum('bqhd,bkhd->bhqk', q.astype(jnp.float32), k.astype(jnp.float32)) * scale
    mask = k_pos[None, :] < q_pos[:, None]
    log_1m = jnp.where(mask, jax.nn.log_sigmoid(-z), 0.0)
    tail = lax.cumsum(log_1m, axis=3, reverse=True) - log_1m
    a = jnp.where(mask, jnp.exp(jax.nn.log_sigmoid(z) + tail), 0.0)
    out = jnp.einsum('bhqk,bkhd->bqhd', a, v.astype(jnp.float32))
    return out.astype(v.dtype)


def sb_prompt(q, k, v):
    b, t, h, d = q.shape
    nb = t // Q_BLOCK
    qb = q.reshape(b, nb, Q_BLOCK, h, d).swapaxes(0, 1)
    pos = jnp.arange(t, dtype=jnp.int32).reshape(nb, Q_BLOCK)
    kpos = jnp.arange(t, dtype=jnp.int32)
    out = lax.map(lambda a: stick_breaking(a[0], k, v, a[1], kpos), (qb, pos))
    return out.swapaxes(0, 1).reshape(b, t, h, d)


def causal_conv(u, buf, w):
    t = u.shape[1]
    up = jnp.concatenate([buf, u], axis=1)
    y = sum(w[j] * up[:, j:j + t] for j in range(CONV_K))
    return y, up[:, -(CONV_K - 1):]


def layer(x, p, attend, conv_buf, g_mix, w_in, conv_w, w_attn_out, w_conv_out, w_o,
          g_ffn, w_up, w_down, g_ple, w_ple_gate, w_ple):
    b, t, _ = x.shape
    h = rmsnorm(x, g_mix)
    proj = h @ w_in
    q, k, v, cb, cc, cx, ga, gc = jnp.split(proj, SPLITS, axis=-1)
    q = q.reshape(b, t, N_HEADS, HEAD_DIM)
    k = k.reshape(b, t, N_HEADS, HEAD_DIM)
    v = v.reshape(b, t, N_HEADS, HEAD_DIM)
    y_attn = attend(q, k, v).reshape(b, t, SB_WIDTH) @ w_attn_out
    conv_out, new_buf = causal_conv(cc * cx, conv_buf, conv_w)
    y_conv = (cb * conv_out) @ w_conv_out
    merged = jax.nn.sigmoid(ga) * y_attn + jax.nn.sigmoid(gc) * y_conv
    x = x + merged @ w_o
    f = jnp.square(jax.nn.relu(rmsnorm(x, g_ffn) @ w_up)) @ w_down
    x = x + f
    x = x + jax.nn.sigmoid(rmsnorm(x, g_ple) @ w_ple_gate) * (p @ w_ple)
    return x, k, v, new_buf


def setup_inputs(seed: int = 0) -> dict:
    key = jax.random.key(seed)
    ks = jax.random.split(key, 24)
    f32 = jnp.float32
    n = lambda k, shape, s: jax.random.normal(k, shape, f32) * s
    return {
        "x_prompt": n(ks[0], (BATCH, SEQ, D_MODEL), 1.0),
        "x_sample": n(ks[1], (DEC_BATCH, DEC_SEQ, D_MODEL), 1.0),
        "p_prompt": n(ks[2], (DEPTH, BATCH, SEQ, PLE_DIM), 1.0),
        "p_sample": n(ks[3], (DEPTH, DEC_BATCH, DEC_SEQ, PLE_DIM), 1.0),
        "cache_k": n(ks[4], (DEPTH, DEC_BATCH, PAST_LEN, N_HEADS, HEAD_DIM), 1.0),
        "cache_v": n(ks[5], (DEPTH, DEC_BATCH, PAST_LEN, N_HEADS, HEAD_DIM), 1.0),
        "cache_conv": n(ks[6], (DEPTH, DEC_BATCH, CONV_K - 1, CONV_WIDTH), 1.0),
        "g_mix": 1.0 + n(ks[7], (DEPTH, D_MODEL), 0.02),
        "w_in": n(ks[8], (DEPTH, D_MODEL, N_PROJ), D_MODEL ** -0.5),
        "conv_w": n(ks[9], (DEPTH, CONV_K, CONV_WIDTH), CONV_K ** -0.5),
        "w_attn_out": n(ks[10], (DEPTH, SB_WIDTH, D_MODEL), SB_WIDTH ** -0.5),
        "w_conv_out": n(ks[11], (DEPTH, CONV_WIDTH, D_MODEL), CONV_WIDTH ** -0.5),
        "w_o": n(ks[12], (DEPTH, D_MODEL, D_MODEL), D_MODEL ** -0.5),
        "g_ffn": 1.0 + n(ks[13], (DEPTH, D_MODEL), 0.02),
        "w_up": n(ks[14], (DEPTH, D_MODEL, D_FF), D_MODEL ** -0.5),
        "w_down": n(ks[15], (DEPTH, D_FF, D_MODEL), D_FF ** -0.5),
        "g_ple": 1.0 + n(ks[16], (DEPTH, D_MODEL), 0.02),
        "w_ple_gate": n(ks[17], (DEPTH, D_MODEL, D_MODEL), D_MODEL ** -0.5),
        "w_ple": n(ks[18], (DEPTH, PLE_DIM, D_MODEL), PLE_DIM ** -0.5),
        "g_final": 1.0 + n(ks[19], (D_MODEL,), 0.02),
    }


def reference(x_prompt, x_sample, p_prompt, p_sample, cache_k, cache_v, cache_conv,
              g_mix, w_in, conv_w, w_attn_out, w_conv_out, w_o,
              g_ffn, w_up, w_down, g_ple, w_ple_gate, w_ple, g_final):
    xp, xs = x_prompt, x_sample
    kp_l, vp_l, cp_l, ks_l, vs_l, cs_l = [], [], [], [], [], []
    for i in range(DEPTH):
        w = (g_mix[i], w_in[i], conv_w[i], w_attn_out[i], w_conv_out[i], w_o[i],
             g_ffn[i], w_up[i], w_down[i], g_ple[i], w_ple_gate[i], w_ple[i])
        buf0 = jnp.zeros((xp.shape[0], CONV_K - 1, CONV_WIDTH), xp.dtype)
        xp, kp, vp, cp = layer(xp, p_prompt[i], sb_prompt, buf0, *w)
        ck, cv = cache_k[i], cache_v[i]
        past = ck.shape[1]

        def sb_sample(q, k, v, ck=ck, cv=cv, past=past):
            t = q.shape[1]
            kk = jnp.concatenate([ck, k], axis=1)
            vv = jnp.concatenate([cv, v], axis=1)
            q_pos = past + jnp.arange(t, dtype=jnp.int32)
            k_pos = jnp.arange(past + t, dtype=jnp.int32)
            return stick_breaking(q, kk, vv, q_pos, k_pos)

        xs, ks_, vs_, cs = layer(xs, p_sample[i], sb_sample, cache_conv[i], *w)
        kp_l.append(kp); vp_l.append(vp); cp_l.append(cp)
        ks_l.append(ks_); vs_l.append(vs_); cs_l.append(cs)
    y_prompt = rmsnorm(xp, g_final)
    y_sample = rmsnorm(xs, g_final)
    return (y_prompt, y_sample,
            jnp.stack(kp_l), jnp.stack(vp_l), jnp.stack(cp_l),
            jnp.stack(ks_l), jnp.stack(vs_l), jnp.stack(cs_l))
```

```python
import numpy as np
import concourse.bass as bass
import concourse.mybir as mybir
from concourse.bass_utils import run_bass_kernel_spmd

F32 = mybir.dt.float32
BF16 = mybir.dt.bfloat16
AF = mybir.ActivationFunctionType
ALU = mybir.AluOpType

D = 1024
NPROJ = 5120
T = 8192
NG = 16
GRP = {0: [0, 3, 4, 7, 8, 11, 12, 15], 1: [1, 2, 5, 6, 9, 10, 13, 14]}
EPS = 1e-6
PAST = 2048

PK, PV, PQ, PCONV, PMIX, PO, PUP, PDN, PPG, PPLE, NPIECE = 0, 1, 2, 3, 7, 15, 17, 25, 33, 35, 37


_STAGE = [9]
_SUB = [0]
_NGRP = [1]
_NCORES = [8]


class Prog:
    def __init__(self, nc):
        self.nc = nc
        self.ins = []
        self.lastw = {}
        self.reads = {}

    def _add(self, eng, fn, reads, writes, dma, semkey):
        idx = len(self.ins)
        deps = set()
        for b in reads:
            if b in self.lastw:
                deps.add(self.lastw[b])
        for b in writes:
            if b in self.lastw:
                deps.add(self.lastw[b])
            for r in self.reads.get(b, ()):
                deps.add(r)
        self.ins.append(dict(eng=eng, fn=fn, deps=deps, dma=dma, semkey=semkey))
        for b in reads:
            self.reads.setdefault(b, []).append(idx)
        for b in writes:
            self.lastw[b] = idx
            self.reads[b] = []
        return idx

    def op(self, eng, fn, reads=(), writes=()):
        return self._add(eng, fn, reads, writes, False, None)

    def dma(self, eng, fn, reads=(), writes=(), semkey=None):
        assert semkey is not None
        return self._add(eng, fn, reads, writes, True, semkey)

    def emit(self):
        nc = self.nc
        engs = dict(pe=nc.tensor, act=nc.scalar, dve=nc.vector, pool=nc.gpsimd, sp=nc.sync)
        ins = self.ins
        needed = set()
        for i, it in enumerate(ins):
            for d in it["deps"]:
                pd = ins[d]
                if pd["dma"] or pd["eng"] != it["eng"] or it["dma"] or pd["eng"] != "pe":
                    needed.add(d)
        sems = {}

        def getsem(k):
            if k not in sems:
                sems[k] = nc.alloc_semaphore("s%d" % len(sems))
            return sems[k]
        cnt = {}
        comp = {}
        waited = {}
        last_dma = {}
        gtot = {}
        for it in ins:
            if it["dma"] and isinstance(it["semkey"], tuple) and it["semkey"][0] == "grp":
                gtot[it["semkey"]] = gtot.get(it["semkey"], 0) + 16
        for i, it in enumerate(ins):
            eng = it["eng"]
            e = engs[eng]
            best = {}
            for d in it["deps"]:
                if d not in comp:
                    continue
                sk, val = comp[d]
                if sk == ("e", "pe") and eng == "pe" and not it["dma"]:
                    continue
                if best.get(sk, 0) < val:
                    best[sk] = val
            for sk, val in best.items():
                if waited.get((eng, sk), 0) >= val:
                    continue
                e.wait_ge(getsem(sk), val)
                waited[(eng, sk)] = val
            r = it["fn"](e)
            if it["dma"]:
                sk = ("d", it["semkey"])
                cnt[sk] = cnt.get(sk, 0) + 16
                r.then_inc(getsem(sk), 16)
                comp[i] = (sk, gtot.get(it["semkey"], cnt[sk]))
                last_dma[sk] = cnt[sk]
            elif i in needed:
                sk = ("e", eng)
                cnt[sk] = cnt.get(sk, 0) + 1
                r.then_inc(getsem(sk), 1)
                comp[i] = (sk, cnt[sk])
        for sk, val in last_dma.items():
            if waited.get(("sp", sk), 0) < val:
                nc.sync.wait_ge(getsem(sk), val)


def build_program():
    nc = bass.Bass("TRN2", target_bir_lowering=False)
    P = Prog(nc)

    def I(eng, meth, reads, writes, **kw):
        P.op(eng, lambda e: getattr(e, meth)(**kw), reads=reads, writes=writes)

    def DM(q, out, in_, reads, writes, semkey):
        P.dma(q, lambda e: e.dma_start(out=out, in_=in_), reads=reads, writes=writes, semkey=semkey)

    def din(name, shape, dt=F32):
        return nc.dram_tensor(name, list(shape), dt, kind="ExternalInput").ap()

    def dout(name, shape, dt=F32):
        return nc.dram_tensor(name, list(shape), dt, kind="ExternalOutput").ap()

    def dscr(name, shape, dt=BF16):
        return nc.dram_tensor(name, list(shape), dt, kind="Internal").ap()

    xall = din("xall", [T, D])
    xown = din("xown", [8, 512, D])
    xhalo = din("xhalo", [8, 2, D])
    pown = din("pown", [8, 512, 256])
    xsam = din("xsam", [128, D])
    psam = din("psam", [128, 256])
    ck = din("ck", [4, PAST, 512])
    cv = din("cv", [4, PAST, 512])
    cconvT = din("cconvT", [512, 8])
    mb = din("mb", [17, 128, 512])
    gfin_d = din("gfin", [128, D])
    convwT = din("convwT", [512, 3])
    gvec_d = din("gvec", [128, 24])
    w_in = din("w_in", [D, NPROJ])
    w_ao = din("w_ao", [512, D])
    w_co = din("w_co", [512, D])
    w_o = din("w_o", [D, D])
    w_up = din("w_up", [D, 4096])
    w_dn = din("w_dn", [4096, D])
    w_pg = din("w_pg", [D, D])
    w_ple = din("w_ple", [256, D])

    y_own = dout("y_own", [8, 512, D])
    k_all = dout("k_all", [T, 512])
    v_all = dout("v_all", [T, 512])
    conv_p = dout("conv_p", [128, 4, 2])
    y_s = dout("y_s", [128, D])
    k_s = dout("k_s", [128, 512])
    v_s = dout("v_s", [128, 512])
    conv_s = dout("conv_s", [128, 4, 4, 2])

    wimg = dscr("wimg", [NPIECE, 128, 4096])
    KTp = dscr("KTp", [8, 64, T])
    Vp = dscr("Vp", [8, 4, 128, 1024])
    KTs = dscr("KTs", [4, 8, 64, 32 + PAST])
    Vs = dscr("Vs", [4, PAST, 512])
    Vsn = dscr("Vsn", [4, 8, 32, 64])
    Vq = dscr("Vq", [8, 4, 128, 1024])
    zrow = dscr("zrow", [1, 512])

    globals_hT = [None]
    globals_xr = [None]
    x_issued = set()
    SB_BASE = 16512
    off = [SB_BASE]

    def sb(name, shape, dt, at=None):
        nbytes = int(np.prod(shape[1:])) * (4 if dt == F32 else 2)
        nbytes = (nbytes + 31) // 32 * 32
        if at is None:
            o = off[0]
            off[0] += nbytes
        else:
            o = at
        assert o + nbytes <= SB_BASE + 212800, (name, o, nbytes)
        return nc.alloc_sbuf_tensor_at(name, list(shape), dt, offset=o)

    ident = sb("ident", [128, 128], BF16)
    zmask = sb("zmask", [128, 512], BF16)
    mbt = sb("mbt", [128, 17, 512], BF16)
    gfin = sb("gfin_s", [128, D], F32)
    cw = sb("cw", [128, 4, 3], F32)
    gvec = sb("gvec_s", [128, 24], F32)
    epsb = sb("epsb", [128, 1], F32)
    ss = sb("ss", [128, 4], F32)
    sq = sb("sq", [128, 4], F32)
    rstd = sb("rstd", [128, 4], F32)
    ssh = sb("ssh", [128, 1], F32)
    sqh = sb("sqh", [128, 1], F32)
    rsh = sb("rsh", [128, 1], F32)
    xr = sb("xr", [128, 4, D], F32)
    hbf = sb("hbf", [128, 4, D], BF16)
    hT = sb("hT", [128, 8, 512], BF16)
    qT = sb("qT", [64, 8, 512], BF16)
    attnT = sb("attnT", [64, 8, 512], BF16)
    ycb = sb("ycb", [128, 4, 512], BF16)
    mT = sb("mT", [128, 8, 512], BF16)
    xh = sb("xh", [2, D], F32)
    hhb = sb("hhb", [2, D], BF16)
    hTh = sb("hTh", [128, 16], BF16)
    cxh = sb("cxh", [128, 2], F32)
    globals_hT[0] = hT
    globals_xr[0] = xr
    wring = [sb("wring%d" % i, [128, 4096], BF16) for i in range(4)]
    cxs = [sb("cxs%d" % i, [128, 512], F32) for i in range(2)]
    ue = [sb("ue%d" % i, [128, 520], F32) for i in range(2)]
    yac = [sb("yac%d" % i, [128, 512], F32) for i in range(2)]
    sga = [cxs[0], cxs[1]]
    sgc = [yac[0], yac[1]]
    t1 = [sb("t1_%d" % i, [128, 512], F32) for i in range(2)]
    stg = [sb("stg%d" % i, [128, D], F32) for i in range(2)]
    junk = sb("junk", [128, D], BF16)
    regB = off[0]
    KTr = [sb("KTr%d" % i, [64, 2048], BF16) for i in range(2)]
    Vr = [sb("Vr%d" % i, [128, 1024], BF16) for i in range(2)]
    Vd = [sb("Vd%d" % i, [128, 1024], BF16) for i in range(2)]
    om = [sb("om%d" % i, [128, 512], F32) for i in range(4)]
    cp = [sb("cp%d" % i, [128, 1026], F32) for i in range(4)]
    Ab = [sb("Ab%d" % i, [128, 528], BF16) for i in range(8)]
    ATb = [sb("ATb%d" % i, [128, 1024], BF16) for i in range(4)]
    KTn = sb("KTn", [64, 32], BF16)
    Vn = sb("Vn", [32, 64], BF16)
    endB = off[0]
    fT = sb("fT", [128, 32, 512], BF16, at=regB)
    gate = sb("gate", [128, 4, 512], F32, at=regB + 32768)
    pst = sb("pst", [128, 4, 256], F32, at=regB + 32768 + 8192)
    pbf = sb("pbf", [128, 4, 256], BF16, at=regB + 32768 + 12288)
    pT = sb("pT", [128, 2, 512], BF16, at=regB + 32768 + 14336)
    assert regB + 32768 + 16384 <= endB
    assert regB + 32768 <= endB, (regB, endB)
    wst = [sb("wst%d" % i, [128, D], F32, at=regB + i * 4096) for i in range(2)]
    wbf = [sb("wbf%d" % i, [128, D], BF16, at=regB + 8192 + i * 2048) for i in range(2)]
    ckb = [sb("ckb%d" % i, [128, 4, 512], BF16, at=regB + 12288 + i * 4096) for i in range(2)]
    xr2 = sb("xr2", [128, 4, D], F32, at=regB + 20480)
    assert regB + 20480 + 16384 <= endB
    REGB = ["KTr0", "KTr1", "Vr0", "Vr1", "Vd0", "Vd1"] + ["om%d" % i for i in range(4)] + \
           ["Ab%d" % i for i in range(8)] + ["ATb%d" % i for i in range(4)] + ["KTn", "Vn"]
    for i in range(4):
        REGB += [("cp%d" % i, "c", 0), ("cp%d" % i, "s", 0), ("cp%d" % i, "s", 1)]
    print("SBUF used", off[0])

    PS = [nc.alloc_psum_tensor("ps%d" % i, [128, 512], F32) for i in range(6)]
    PT = [nc.alloc_psum_tensor("pt%d" % i, [128, 1024], BF16) for i in range(2)]
    rr_ = dict(ps=0, pt=0, ev=0, at=0, kv=0, pr=0, om=0, sg=0)

    def psb():
        i = rr_["ps"] % 6
        rr_["ps"] += 1
        return PS[i], "ps%d" % i

    def ptb():
        i = rr_["pt"] % 2
        rr_["pt"] += 1
        return PT[i], ("pt", i)

    def evac_eng():
        rr_["ev"] += 1
        return "act" if rr_["ev"] % 2 else "dve"

    def copy_op(eng, out, in_, reads, writes):
        if eng == "act":
            I("act", "copy", reads, writes, out=out, in_=in_)
        else:
            I(eng, "tensor_copy", reads, writes, out=out, in_=in_)

    I("pool", "memset", [], ["ident"], ap=ident[:], constant=0.0)
    I("pool", "affine_select", ["ident"], ["ident"], out=ident[:], in_=ident[:], pattern=[[-1, 128]],
      compare_op=ALU.not_equal, fill=1.0, base=0, channel_multiplier=1)
    I("pool", "memset", [], ["zmask"], ap=zmask[:], constant=0.0)
    I("pool", "memset", [], ["epsb"], ap=epsb[:], constant=EPS)
    DM("pool", mbt[:], mb.rearrange("m p k -> p m k"), [], ["mbt"], "mbt")
    DM("sp", gfin[:], gfin_d, [], ["gfin"], "gfin")
    DM("sp", cw[:], convwT.rearrange("(c p) j -> p c j", p=128), [], ["cw"], "cw")
    DM("sp", gvec[:], gvec_d, [], ["gvec"], "gvec")
    vq_keys = {rb: [] for rb in range(4)}
    DM("sp", zrow, zmask[0:1, 0:512], ["zmask"], ["zrow"], "zrow")

    parts = {}

    def wpart(piece, part):
        parts.setdefault(piece, set()).add(part)
        return ("wimg", piece, part)

    def img(piece, kc, ncol, c0, w):
        return wimg[piece, :, kc * ncol + c0: kc * ncol + c0 + w]

    def prep_unscaled():
        for j in range(2):
            src = w_o.rearrange("(k p) (j c) -> j p k c", p=128, j=2)[j]
            dst = wimg[PO + j].rearrange("p (k c) -> p k c", k=8)
            DM("pool", dst, src, [], [wpart(PO + j, 0)], ("grp", "pu"))
        for ch in range(2):
            for kq in range(4):
                pc = PDN + ch * 4 + kq
                src = w_dn[kq * 1024:(kq + 1) * 1024, ch * 512:(ch + 1) * 512].rearrange("(k p) c -> p k c", p=128)
                dst = wimg[pc].rearrange("p (k c) -> p k c", k=8)
                DM("pool", dst, src, [], [wpart(pc, 0)], ("grp", "pu"))
        for j in range(2):
            src = w_ple[:, j * 512:(j + 1) * 512].rearrange("(k p) c -> p k c", p=128)
            dst = wimg[PPLE + j, :, 0:1024].rearrange("p (k c) -> p k c", k=2)
            DM("pool", dst, src, [], [wpart(PPLE + j, 0)], ("grp", "pu"))
        for c in range(8):
            src = w_ao[:, c * 128:(c + 1) * 128].rearrange("(h q) m -> q h m", q=64)
            dst = wimg[PMIX + c, 0:64, 2048:3072].rearrange("q (h m) -> q h m", h=8)
            DM("pool", dst, src, [], [wpart(PMIX + c, "ao")], ("grp", "pu"))
            dst2 = wimg[PMIX + c, 64:128, 2048:3072].rearrange("q (h m) -> q h m", h=8)
            DM("pool", dst2, src, [], [wpart(PMIX + c, "ao2")], ("grp", "pu"))
            src = w_co[:, c * 128:(c + 1) * 128].rearrange("(k p) m -> p k m", p=128)
            dst = wimg[PMIX + c, :, 3072:3584].rearrange("p (k m) -> p k m", k=4)
            DM("pool", dst, src, [], [wpart(PMIX + c, "co")], ("grp", "pu"))

    prep_items = []
    prep_loaded = set()
    multi_keys = {"conv": [], "mix": []}

    prep_specs = []

    def scaled_item(W, kc, c0, gsel, stores):
        idx = len(prep_specs)
        prep_specs.append((W, kc, c0))

        def issue_load(k):
            if k >= len(prep_specs) or k in prep_loaded:
                return
            prep_loaded.add(k)
            W_, kc_, c0_ = prep_specs[k]
            i_ = k % 2
            DM("pool", wst[i_][:], W_[kc_ * 128:(kc_ + 1) * 128, c0_:c0_ + 1024], [], ["wst%d" % i_], ("wst", i_))

        def run():
            i = idx % 2
            issue_load(idx)
            issue_load(idx + 1)
            I("pool", "tensor_scalar", ["wst%d" % i, "gvec"], ["wbf%d" % i], out=wbf[i][:], in0=wst[i][:],
              scalar1=gvec[:, gsel * 8 + kc: gsel * 8 + kc + 1], scalar2=None, op0=ALU.mult)
            for n_, (sl, dst, pk) in enumerate(stores):
                DM("pool", dst, sl(wbf[i]), ["wbf%d" % i], [pk], ("wbf", i, n_))
        prep_items.append(run)

    def mkey(kind, kc, t_):
        k = ("wimg", kind, kc, t_)
        multi_keys[kind].append(k)
        return k

    def build_prep_items():
        for kc in range(8):
            scaled_item(w_in, kc, 0, 0, [
                (lambda t: t[:, 512:1024], img(PK, kc, 512, 0, 512), wpart(PK, kc)),
                (lambda t: t[:, 0:512], img(PQ, kc, 512, 0, 512), wpart(PQ, kc)),
            ])
            scaled_item(w_in, kc, 1024, 0, [
                (lambda t: t[:, 0:512], img(PV, kc, 512, 0, 512), wpart(PV, kc)),
                (lambda t: t[:, 512:1024].rearrange("p (c i) -> p c i", c=4),
                 wimg[PCONV:PCONV + 4, :, kc * 384: kc * 384 + 128].rearrange("c p i -> p c i"), mkey("conv", kc, 0)),
            ])
        for kc in range(8):
            scaled_item(w_in, kc, 2048, 0, [
                (lambda t: t[:, 0:512].rearrange("p (c i) -> p c i", c=4),
                 wimg[PCONV:PCONV + 4, :, kc * 384 + 128: kc * 384 + 256].rearrange("c p i -> p c i"), mkey("conv", kc, 1)),
                (lambda t: t[:, 512:1024].rearrange("p (c i) -> p c i", c=4),
                 wimg[PCONV:PCONV + 4, :, kc * 384 + 256: kc * 384 + 384].rearrange("c p i -> p c i"), mkey("conv", kc, 2)),
            ])
        for kc in range(8):
            for t_ in range(2):
                scaled_item(w_in, kc, 3072 + t_ * 1024, 0, [
                    (lambda t: t[:, :].rearrange("p (c i) -> p c i", c=8),
                     wimg[PMIX:PMIX + 8, :, kc * 256 + t_ * 128: kc * 256 + t_ * 128 + 128].rearrange("c p i -> p c i"), mkey("mix", kc, t_)),
                ])
        for kc in range(8):
            for s_ in range(4):
                scaled_item(w_up, kc, s_ * 1024, 1, [
                    (lambda t: t[:, 0:512], img(PUP + 2 * s_, kc, 512, 0, 512), wpart(PUP + 2 * s_, kc)),
                    (lambda t: t[:, 512:1024], img(PUP + 2 * s_ + 1, kc, 512, 0, 512), wpart(PUP + 2 * s_ + 1, kc)),
                ])
        for kc in range(8):
            scaled_item(w_pg, kc, 0, 2, [
                (lambda t: t[:, 0:512], img(PPG, kc, 512, 0, 512), wpart(PPG, kc)),
                (lambda t: t[:, 512:1024], img(PPG + 1, kc, 512, 0, 512), wpart(PPG + 1, kc)),
            ])

    class WS:
        def __init__(self):
            self.seq = []
            self.issued = 0
            self.used = 0

        def plan(self, pieces):
            self.seq.extend(pieces)

        def _issue(self):
            k = self.issued
            piece, ncol = self.seq[k]
            slot = k % 4
            rd = [("wimg", piece, p) for p in parts.get(piece, ())]
            if PCONV <= piece < PCONV + 4:
                rd += multi_keys["conv"]
            if PMIX <= piece < PMIX + 8:
                rd += multi_keys["mix"]
            DM("sp", wring[slot][:, 0:ncol], wimg[piece, :, 0:ncol], rd, ["wring%d" % slot], ("wring", slot))
            self.issued += 1

        def next(self):
            k = self.used
            while self.issued < min(len(self.seq), k + 4):
                self._issue()
            self.used += 1
            slot = k % 4
            return wring[slot], "wring%d" % slot

    ws = WS()

    def norm_stats(NS_, xr=xr, xk="xr"):
        for s in range(NS_):
            I("act", "activation", [(xk, s)], ["junk", "ss"], out=junk[:], in_=xr[:, s, :], func=AF.Square, accum_out=ss[:, s:s + 1])
        I("act", "activation", ["ss", "epsb"], ["sq"], out=sq[:, 0:NS_], in_=ss[:, 0:NS_], func=AF.Sqrt, bias=epsb[:, 0:1], scale=1.0 / D)
        I("dve", "reciprocal", ["sq"], ["rstd"], out=rstd[:, 0:NS_], in_=sq[:, 0:NS_])

    def norm_to_hT(NS_, NT_, hT=hT, hk="hT", xr=xr, xk="xr"):
        norm_stats(NS_, xr, xk)
        for s in range(NS_):
            I("dve", "tensor_scalar", [(xk, s), "rstd"], [("hbf", s)], out=hbf[:, s, :], in0=xr[:, s, :], scalar1=rstd[:, s:s + 1], scalar2=None, op0=ALU.mult)
        import os
        npair = int(os.environ.get("DBG_NPAIR", "4"))
        dbg_ev = os.environ.get("DBG_EV", "")
        for kp in range(npair):
            pt, pk = ptb()
            for kk in range(2):
                kc = kp * 2 + kk
                for s in range(NS_):
                    I("pe", "transpose", [("hbf", s), "ident"], [pk], out=pt[:, kk * 512 + s * 128: kk * 512 + (s + 1) * 128],
                      in_=hbf[:, s, kc * 128:(kc + 1) * 128], identity=ident[:])
            if dbg_ev == "none":
                continue
            copy_op(dbg_ev or evac_eng(), hT[:, kp * 2:kp * 2 + 2, 0:NT_], pt[:, :].rearrange("p (k t) -> p k t", k=2)[:, :, 0:NT_], [pk],
                    [(hk, kp * 2), (hk, kp * 2 + 1)])

    def mm_acc(bank_ap, bk, pairs, extra_reads):
        n = len(pairs)
        for i, (l, r, rk) in enumerate(pairs):
            I("pe", "matmul", list(rk) + list(extra_reads), [bk], out=bank_ap, lhsT=l, rhs=r, start=(i == 0), stop=(i == n - 1))

    def phase1a(kind, g):
        NT_ = 512 if kind == "P" else 128
        NS_ = NT_ // 128
        hT, hk = (globals_hT[0], "hT") if g % 2 == 0 else (mT, "mT")

        def stgr():
            i = rr_["sg"] % 4
            rr_["sg"] += 1
            return stg[i // 2][:, (i % 2) * 512:(i % 2 + 1) * 512], ("stgq", i)
        def xload(kind_, g_):
            xt, xk_ = (globals_xr[0], "xr") if (g_ % 2 == 0) else (xr2, "xr2")
            if (kind_, g_) in x_issued:
                return xt, xk_
            x_issued.add((kind_, g_))
            for s in range(4 if kind_ == "P" else 1):
                src = xall[g_ * 512 + s * 128: g_ * 512 + (s + 1) * 128, :] if kind_ == "P" else xsam[:, :]
                DM("sp", xt[:, s, :], src, [], [(xk_, s)], (xk_, s))
            return xt, xk_
        gi = g if kind == "P" else 16
        xr, xk = xload(kind, gi)
        if kind == "P":
            if g + 1 < NG:
                xload("P", g + 1)
            else:
                xload("S", 16)
        SUB = _SUB[0]
        import os
        dbs = int(os.environ.get("DBG_S", "9")) if kind == "S" else 9
        if SUB == 1:
            norm_stats(NS_, xr, xk)
            return
        norm_to_hT(NS_, NT_, hT, hk, xr, xk)
        if SUB == 2 or dbs == 1:
            return
        Wk, wkk = ws.next()
        Wk3 = Wk[:, :].rearrange("p (k c) -> p k c", k=8)
        if SUB == 3:
            return
        for h in range(8):
            bank, bk = psb()
            mm_acc(bank[0:64, 0:NT_], bk, [(Wk3[:, kc, h * 64:(h + 1) * 64], hT[:, kc, 0:NT_], [(hk, kc)]) for kc in range(8)], [wkk])
            copy_op(evac_eng(), attnT[0:64, h, 0:NT_], bank[0:64, 0:NT_], [bk], ["attnT"])
        if SUB == 4 or dbs == 2:
            return
        if kind == "P":
            DM("sp", KTp[:, :, g * 512:(g + 1) * 512].rearrange("h d t -> d h t"), attnT[0:64, :, :], ["attnT"], [("KTp", g)], "ktst")
        else:
            for i in range(4):
                DM("sp", KTs[i, :, :, 0:32].rearrange("h d t -> d h t"), attnT[0:64, :, i * 32:(i + 1) * 32], ["attnT"], [("KTs", i, "n")], ("ktst", i))
        if SUB == 5 or dbs == 3:
            return
        for s in range(NS_):
            bank, bk = psb()
            mm_acc(bank[:, :], bk, [(hT[:, kc, s * 128:(s + 1) * 128], Wk3[:, kc, :], [(hk, kc)]) for kc in range(8)], [wkk])
            sgt, sgk = stgr()
            copy_op(evac_eng(), sgt, bank[:, :], [bk], [sgk])
            dst = k_all[g * 512 + s * 128: g * 512 + (s + 1) * 128, :] if kind == "P" else k_s[:, :]
            DM("sp", dst, sgt, [sgk], [], sgk)
        if SUB == 6 or dbs == 4:
            return
        Wv, wvk = ws.next()
        Wv3 = Wv[:, :].rearrange("p (k c) -> p k c", k=8)
        for s in range(NS_):
            bank, bk = psb()
            mm_acc(bank[:, :], bk, [(hT[:, kc, s * 128:(s + 1) * 128], Wv3[:, kc, :], [(hk, kc)]) for kc in range(8)], [wvk])
            sgt, sgk = stgr()
            copy_op("act", sgt, bank[:, :], [bk], [sgk])
            copy_op("dve", ycb[:, s, :], sgt, [sgk], [("ycb", s)])
            dst = v_all[g * 512 + s * 128: g * 512 + (s + 1) * 128, :] if kind == "P" else v_s[:, :]
            DM("sp", dst, sgt, [sgk], [], sgk)
            if dbs == 5:
                continue
            if kind == "P":
                c = (g % 4) * 4 + s
                dstv = Vp[:, g // 4, :, c * 64:(c + 1) * 64].rearrange("h p d -> p h d")
                DM("sp", dstv, ycb[:, s, :].rearrange("p (h d) -> p h d", h=8), [("ycb", s)], [("Vp", g // 4, g % 4, s)], ("vst", s))
            else:
                for i in range(4):
                    DM("sp", Vsn[i].rearrange("h t d -> t h d"), ycb[i * 32:(i + 1) * 32, 0, :].rearrange("p (h d) -> p h d", h=8),
                       [("ycb", 0)], [("Vsn", i)], ("vsn", i))

    def cache_prep():
        for i in range(4):
            for hf in range(2):
                DM("pool", Vs[i, hf * 1024:(hf + 1) * 1024, :], cv[i, hf * 1024:(hf + 1) * 1024, :], [], [("Vs", i, hf)], ("grp", "cvs"))
        n = 0
        for i in range(4):
            for blk in range(4):
                b = n % 2
                n += 1
                DM("pool", ckb[b][:], ck[i, blk * 512:(blk + 1) * 512, :].rearrange("(c p) f -> p c f", p=128), [], ["ckb%d" % b], ("ckb", b))
                for hp in range(4):
                    pt, pk = ptb()
                    for hh in range(2):
                        h = hp * 2 + hh
                        for c in range(4):
                            I("pe", "transpose", ["ckb%d" % b, "ident"], [pk], out=pt[0:64, hh * 512 + c * 128: hh * 512 + (c + 1) * 128],
                              in_=ckb[b][:, c, h * 64:(h + 1) * 64], identity=ident[:])
                    copy_op(evac_eng(), attnT[0:64, hp * 2:hp * 2 + 2, :], pt[0:64, :].rearrange("p (k t) -> p k t", k=2), [pk], ["attnT"])
                DM("sp", KTs[i, :, :, 32 + blk * 512: 32 + (blk + 1) * 512].rearrange("h d t -> d h t"), attnT[0:64, :, :],
                   ["attnT"], [("KTs", i, blk)], "ktst")

    def attention(kind, j):
        chains = []
        if kind == "P":
            kb0 = 2 * j
            jpar = j % 2
            for h in range(8):
                blocks = []
                for i, kb in enumerate(range(kb0, 16)):
                    if i == 0:
                        mk = [mbt[:, (jpar * 2 + 0) * 4 + r, :] for r in range(4)]
                    elif i == 1:
                        mk = [mbt[:, (jpar * 2 + 1) * 4 + r, :] for r in range(4)]
                    else:
                        mk = [zmask[:, :]] * 4
                    blocks.append(dict(nk=512, src=("P", h, kb // 4), sub=kb % 4, mask=mk))
                chains.append(dict(h=h, q0=0, R=4, nq=128, blocks=blocks))
        else:
            for i in range(4):
                for h in range(8):
                    blocks = [dict(nk=32, src=("N", i, h), sub=0, mask=[mbt[0:32, 16, 0:32]])]
                    for blk in range(4):
                        blocks.append(dict(nk=512, src=("S", i, h), sub=blk, mask=[zmask[0:32, :]]))
                    chains.append(dict(h=h, q0=i * 32, R=1, nq=32, blocks=blocks))
        steps = [(ci, bi) for ci, ch in enumerate(chains) for bi in range(len(ch["blocks"]))]
        cur_src = [None, None]
        pending = None
        src_order = []
        for ch in chains:
            for blk in ch["blocks"]:
                if blk["src"][0] != "N" and blk["src"] not in src_order:
                    src_order.append(blk["src"])
        kv_slot = {}

        def issue_kv(src):
            slot = rr_["kv"] % 2
            rr_["kv"] += 1
            kv_slot[src] = slot
            V3d = Vr[slot][:, :].rearrange("p (c d) -> p c d", c=16)
            D3d = Vd[slot][:, :].rearrange("p (c d) -> p c d", c=16)
            if src[0] == "P":
                _, h_, rb = src
                DM("sp", KTr[slot][:, :], KTp[h_, :, rb * 2048:(rb + 1) * 2048],
                   [("KTp", g_) for g_ in range(rb * 4, rb * 4 + 4)], ["KTr%d" % slot], ("KTr", slot))
                DM("sp", Vr[slot][:, :], Vp[h_, rb],
                   [("Vp", rb, a, b_) for a in range(4) for b_ in range(4)], ["Vr%d" % slot], ("Vr", slot))
                vp_rd = [("Vp", rb, a, b_) for a in range(4) for b_ in range(4)]
                DM("sp", Vd[slot][0:127, :], Vp[h_, rb, 1:128, :], vp_rd, ["Vd%d" % slot, ("Vd%d" % slot, 1)], ("Vd", slot))
                DM("sp", Vd[slot][127:128, 0:960], Vp[h_, rb, 0:1, 64:1024], vp_rd, [("Vd%d" % slot, 2)], ("Vd", slot))
                if rb < 3:
                    DM("sp", Vd[slot][127:128, 960:1024], Vp[h_, rb + 1, 0:1, 0:64], [("Vp", rb + 1, 0, 0)], [("Vd%d" % slot, 3)], ("Vd", slot))
                else:
                    DM("sp", Vd[slot][127:128, 960:1024], zrow[:, 0:64], ["zrow"], [("Vd%d" % slot, 4)], ("Vd", slot))
            else:
                _, i_, h_ = src
                DM("sp", KTr[slot][:, :], KTs[i_, h_, :, 32:32 + PAST],
                   [("KTs", i_, b_) for b_ in range(4)], ["KTr%d" % slot], ("KTr", slot))
                for hf in range(2):
                    DM("sp", V3d[:, hf * 8:(hf + 1) * 8, :],
                       Vs[i_, hf * 1024:(hf + 1) * 1024, h_ * 64:(h_ + 1) * 64].rearrange("(c p) d -> p c d", p=128),
                       [("Vs", i_, hf)], ["Vr%d" % slot], ("Vr", slot))
                DM("sp", D3d[:, 0:8, :], Vs[i_, 1:1025, h_ * 64:(h_ + 1) * 64].rearrange("(c p) d -> p c d", p=128),
                   [("Vs", i_, 0), ("Vs", i_, 1)], ["Vd%d" % slot, ("Vd%d" % slot, 5)], ("Vd", slot))
                DM("sp", D3d[:, 8:15, :], Vs[i_, 1025:1921, h_ * 64:(h_ + 1) * 64].rearrange("(c p) d -> p c d", p=128),
                   [("Vs", i_, 1)], [("Vd%d" % slot, 6)], ("Vd", slot))
                DM("sp", D3d[0:127, 15, :], Vs[i_, 1921:2048, h_ * 64:(h_ + 1) * 64], [("Vs", i_, 1)], [("Vd%d" % slot, 7)], ("Vd", slot))
                DM("sp", D3d[127:128, 15, :], zrow[:, 0:64], ["zrow"], [("Vd%d" % slot, 8)], ("Vd", slot))
            I("pool", "tensor_tensor", ["Vr%d" % slot] + [("Vd%d" % slot, n_) for n_ in range(1, 10)], ["Vd%d" % slot] + [("Vd%d" % slot, n_) for n_ in range(1, 10)], out=Vd[slot][:, :], in0=Vd[slot][:, :], in1=Vr[slot][:, :], op=ALU.subtract)

        pendingC = None
        pf_queue = []
        for st in range(len(steps) + 2):
            newp = None
            want_prefetch = None
            if st < len(steps):
                ci, bi = steps[st]
                ch = chains[ci]
                blk = ch["blocks"][bi]
                R, nq, nk, h = ch["R"], ch["nq"], blk["nk"], ch["h"]
                rbase = (ci % 4) if kind == "S" else 0
                src = blk["src"]
                if src[0] == "N":
                    _, i_, h_ = src
                    DM("sp", KTn[:, :], KTs[i_, h_, :, 0:32], [("KTs", i_, "n")], ["KTn"], "KTn")
                    DM("sp", Vn[:, :], Vsn[i_, h_], [("Vsn", i_)], ["Vn"], "Vn")
                    kt_ap = KTn[0:64, 0:32]
                    ktk, vk = "KTn", "Vn"
                    vch = [(Vn[0:32, :], 32)]
                    abel = False
                    vrow0 = None
                else:
                    if src not in kv_slot:
                        issue_kv(src)
                    k_ = src_order.index(src)
                    if k_ + 1 < len(src_order) and src_order[k_ + 1] not in kv_slot:
                        want_prefetch = src_order[k_ + 1]
                    cur_src = [src, kv_slot[src]]
                    slot = cur_src[1]
                    sub = blk["sub"]
                    kt_ap = KTr[slot][0:64, sub * 512:(sub + 1) * 512]
                    ktk, vk = "KTr%d" % slot, "Vr%d" % slot
                    V3 = Vr[slot][:, :].rearrange("p (c d) -> p c d", c=16)
                    D3 = Vd[slot][:, :].rearrange("p (c d) -> p c d", c=16)
                    abel = bi >= 2
                    if abel:
                        vch = [(D3[:, sub * 4 + c, :], 128) for c in range(4)]
                        vk = "Vd%d" % slot
                    else:
                        vch = [(V3[:, sub * 4 + c, :], 128) for c in range(4)]
                    vrow0 = (V3[0:1, sub * 4, :], "Vr%d" % slot)
                s_ = bi % 2
                ic = 512 * s_ + 512 - nk
                nblk = len(ch["blocks"])
                abufs = []
                casts = []
                for r in range(R):
                    rr = (rbase + r) % 4
                    cpt, cpk = cp[rr], "cp%d" % rr
                    if bi == 0:
                        I("dve", "memset", [], [(cpk, "c", s_), (cpk, "s", s_)], ap=cpt[:, ic:ic + 1], constant=1.0)
                    bank, bk = PS[rr], "ps%d" % rr
                    qa = qT[0:64, h, ch["q0"] + r * nq: ch["q0"] + (r + 1) * nq]
                    I("pe", "matmul", ["qT", ktk], [bk], out=bank[0:nq, 0:nk], lhsT=qa, rhs=kt_ap, start=True, stop=True)
                    oi = rr_["om"] % 4
                    rr_["om"] += 1
                    omt, omk = om[oi], "om%d" % oi
                    I("act", "activation", [bk], [omk], out=omt[0:nq, 0:nk], in_=bank[0:nq, 0:nk], func=AF.Sigmoid, scale=-0.125)
                    rdc = [(cpk, "c", s_)] if (bi == 0 or s_ == 0) else [(cpk, "s", 1 - s_)]
                    I("dve", "tensor_tensor_scan", [omk, "mbt", "zmask"] + rdc, [(cpk, "s", s_)],
                      out=cpt[0:nq, ic + 1: ic + 1 + nk], data0=omt[0:nq, 0:nk], data1=blk["mask"][r], initial=cpt[0:nq, ic:ic + 1],
                      op0=ALU.mult, op1=ALU.max)
                    ai = s_ * 4 + rr
                    at_, ak = Ab[ai], "Ab%d" % ai
                    if abel:
                        casts.append((cpk, s_, at_, ak, cpt[0:nq, ic + 1: ic + 1 + nk], at_[0:nq, 0:nk]))
                    else:
                        I("dve", "tensor_tensor", [(cpk, "s", s_)] + rdc, [ak], out=at_[0:nq, 0:nk], in0=cpt[0:nq, ic: ic + nk],
                          in1=cpt[0:nq, ic + 1: ic + 1 + nk], op=ALU.subtract)
                        if bi == 1 and nblk > 2:
                            I("dve", "tensor_copy", [(cpk, "s", s_), ak], [ak], out=at_[0:nq, 512:513], in_=cpt[0:nq, ic + nk: ic + nk + 1])
                    if s_ == 1 and bi + 1 < nblk:
                        I("pool", "tensor_copy", [(cpk, "s", 1)], [(cpk, "c", 0)], out=cpt[0:nq, 0:1], in_=cpt[0:nq, 1024:1025])
                    abufs.append((at_, ak))
                newp = dict(q0=ch["q0"], bi=bi, nblk=nblk, abufs=abufs, vch=vch, vk=vk, R=R, nq=nq, nk=nk, h=h, ci=ci, abel=abel,
                            vrow0=vrow0, casts=casts)
                if pending is not None and pending["bi"] == 1 and pending["nblk"] > 2 and pending["ci"] == ci:
                    pending["next_vrow0"] = vrow0
            if pendingC is not None:
                for (rd_, wk_, kw_) in pendingC["mml"]:
                    I("pe", "matmul", rd_, [wk_], **kw_)
                if pendingC["fin"] is not None:
                    o_, i_ap, k_ = pendingC["fin"]
                    copy_op("dve", o_, i_ap, [k_], ["attnT"])
            newC = None
            if pending is not None:
                pd = pending
                R2, nq2, h2 = pd["R"], pd["nq"], pd["h"]
                acc, acck = (PS[4], "ps4") if pd["ci"] % 2 == 0 else (PS[5], "ps5")
                ncs = len(pd["vch"])
                mml = []
                for (cpk_, s__, at__, ak_, src_ap, dst_ap) in pd["casts"]:
                    I("act", "copy", [(cpk_, "s", s__)], [ak_], out=dst_ap, in_=src_ap)
                for cp_ in range((ncs + 1) // 2):
                    pt, pk = ptb()
                    cl = [c for c in (2 * cp_, 2 * cp_ + 1) if c < ncs]
                    for ci_, c in enumerate(cl):
                        va, kp = pd["vch"][c]
                        for r in range(R2):
                            at_, ak = pd["abufs"][r]
                            I("pe", "transpose", [ak, "ident"], [pk], out=pt[0:kp, ci_ * 512 + r * nq2: ci_ * 512 + (r + 1) * nq2],
                              in_=at_[0:nq2, c * 128: c * 128 + kp], identity=ident[0:nq2, 0:nq2])
                    ati = rr_["at"] % 4
                    rr_["at"] += 1
                    att, atk = ATb[ati], "ATb%d" % ati
                    kp0 = pd["vch"][cl[0]][1]
                    ev_e = "dve" if (pd["abel"] and cp_ == 1 and pd["bi"] % 2 == 0) else "act"
                    if len(cl) == 2:
                        copy_op(ev_e, att[0:kp0, :].rearrange("p (k t) -> p k t", k=2)[:, :, 0:R2 * nq2],
                                pt[0:kp0, :].rearrange("p (k t) -> p k t", k=2)[:, :, 0:R2 * nq2], [pk], [atk])
                    else:
                        copy_op("act", att[0:kp0, 0:R2 * nq2], pt[0:kp0, 0:R2 * nq2], [pk], [atk])
                    for ci_, c in enumerate(cl):
                        va, kp = pd["vch"][c]
                        first = (pd["bi"] == 0 and c == 0)
                        last = (pd["bi"] == pd["nblk"] - 1 and c == ncs - 1)
                        mml.append(([pd["vk"], atk], acck, dict(out=acc[0:64, 0:R2 * nq2], lhsT=va, rhs=att[0:kp, ci_ * 512: ci_ * 512 + R2 * nq2], start=first, stop=last)))
                if pd.get("next_vrow0") is not None:
                    vr_ap, vr_k = pd["next_vrow0"]
                    pt, pk = ptb()
                    for r in range(R2):
                        at_, ak = pd["abufs"][r]
                        I("pe", "transpose", [ak, "ident"], [pk], out=pt[0:1, r * nq2:(r + 1) * nq2], in_=at_[0:nq2, 512:513], identity=ident[0:nq2, 0:nq2])
                    ati = rr_["at"] % 4
                    rr_["at"] += 1
                    att, atk = ATb[ati], "ATb%d" % ati
                    copy_op("act", att[0:1, 0:R2 * nq2], pt[0:1, 0:R2 * nq2], [pk], [atk])
                    mml.append(([vr_k, atk], acck, dict(out=acc[0:64, 0:R2 * nq2], lhsT=vr_ap, rhs=att[0:1, 0:R2 * nq2], start=False, stop=False)))
                fin = None
                if pd["bi"] == pd["nblk"] - 1:
                    q0 = pd["q0"]
                    fin = (attnT[0:64, h2, q0: q0 + R2 * nq2], acc[0:64, 0:R2 * nq2], acck)
                newC = dict(mml=mml, fin=fin)
            pendingC = newC
            pending = newp
            if want_prefetch is not None:
                pf_queue.append((st + 2, want_prefetch))
            while pf_queue and pf_queue[0][0] <= st:
                _, src_ = pf_queue.pop(0)
                if src_ not in kv_slot:
                    issue_kv(src_)

    def group(kind, j):
        NT_ = 512 if kind == "P" else 128
        NS_ = NT_ // 128
        L = 512 if kind == "P" else 32

        def uev(t, a, b):
            if kind == "P":
                return t[:, a:b]
            return t[:, 0:136].rearrange("p (s l) -> p s l", s=4)[:, :, a:b]

        def v3(ap2):
            if kind == "P":
                return ap2
            return ap2.rearrange("p (s l) -> p s l", s=4)
        for s in range(NS_):
            src = xown[j, s * 128:(s + 1) * 128, :] if kind == "P" else xsam[:, :]
            DM("sp", xr[:, s, :], src, [], [("xr", s)], ("xr", s))
        norm_to_hT(NS_, NT_)
        Wq, wqk = ws.next()
        Wq3 = Wq[:, :].rearrange("p (k c) -> p k c", k=8)
        for h in range(8):
            bank, bk = psb()
            mm_acc(bank[0:64, 0:NT_], bk, [(Wq3[:, kc, h * 64:(h + 1) * 64], hT[:, kc, 0:NT_], [("hT", kc)]) for kc in range(8)], [wqk])
            copy_op(evac_eng(), qT[0:64, h, 0:NT_], bank[0:64, 0:NT_], [bk], ["qT"])
        hTh3 = hTh[:, :].rearrange("p (k t) -> p k t", k=8)
        if kind == "P":
            DM("sp", xh[:, :], xhalo[j], [], ["xh"], "xh")
            I("act", "activation", ["xh"], ["junk", "ssh"], out=junk[0:2, :], in_=xh[:, :], func=AF.Square, accum_out=ssh[0:2, :])
            I("act", "activation", ["ssh", "epsb"], ["sqh"], out=sqh[0:2, :], in_=ssh[0:2, :], func=AF.Sqrt, bias=epsb[0:2, 0:1], scale=1.0 / D)
            I("dve", "reciprocal", ["sqh"], ["rsh"], out=rsh[0:2, :], in_=sqh[0:2, :])
            I("dve", "tensor_scalar", ["xh", "rsh"], ["hhb"], out=hhb[:, :], in0=xh[:, :], scalar1=rsh[0:2, 0:1], scalar2=None, op0=ALU.mult)
            pt, pk = ptb()
            for kc in range(8):
                I("pe", "transpose", ["hhb", "ident"], [pk], out=pt[:, kc * 2: kc * 2 + 2], in_=hhb[0:2, kc * 128:(kc + 1) * 128], identity=ident[0:2, 0:2])
            copy_op("dve", hTh[:, :], pt[:, 0:16], [pk], ["hTh"])
        for c in range(4):
            Wc, wck = ws.next()
            Wc3 = Wc[:, 0:3072].rearrange("p (k c) -> p k c", k=8)
            banks = [psb() for _ in range(3)]
            for t_ in range(3):
                mm_acc(banks[t_][0][:, 0:NT_], banks[t_][1], [(Wc3[:, kc, t_ * 128:(t_ + 1) * 128], hT[:, kc, 0:NT_], [("hT", kc)]) for kc in range(8)], [wck])
            i2 = c % 2
            uet, uek = ue[i2], "ue%d" % i2
            if kind == "P":
                bh, bhk = psb()
                for t_ in (1, 2):
                    mm_acc(bh[:, (t_ - 1) * 2:(t_ - 1) * 2 + 2], bhk, [(Wc3[:, kc, t_ * 128:(t_ + 1) * 128], hTh3[:, kc, :], ["hTh"]) for kc in range(8)], [wck])
                I("act", "copy", [bhk], ["cxh"], out=cxh[:, :], in_=bh[:, 2:4])
                I("dve", "tensor_tensor", [bhk, "cxh"], [(uek, "h")], out=uet[:, 512:514], in0=bh[:, 0:2], in1=cxh[:, :], op=ALU.mult)
            else:
                DM("sp", uev(uet, 32, 34), cconvT[c * 128:(c + 1) * 128, :].rearrange("p (s t) -> p s t", s=4), [], [(uek, "h")], ("ueh", i2))
            (bcb, bcbk), (bcc, bcck), (bcx, bcxk) = banks
            I("act", "copy", [bcxk], ["cxs%d" % i2], out=cxs[i2][:, 0:NT_], in_=bcx[:, 0:NT_])
            I("dve", "tensor_tensor", [bcck, "cxs%d" % i2], [(uek, "m")], out=uev(uet, 0, L), in0=v3(bcc[:, 0:NT_]), in1=v3(cxs[i2][:, 0:NT_]), op=ALU.mult)
            if kind == "S":
                DM("sp", conv_s[:, c, :, :], uev(uet, 0, 2), [(uek, "m")], [], ("ueo", i2))
            elif j == 0:
                DM("sp", conv_p[:, c, :], uet[:, 0:2], [(uek, "m")], [], ("ueo", i2))
            yt, yk = yac[i2], "yac%d" % i2
            I("dve", "tensor_scalar", [(uek, "m"), "cw"], [yk], out=v3(yt[:, 0:NT_]), in0=uev(uet, 0, L), scalar1=cw[:, c, 2:3], scalar2=None, op0=ALU.mult)
            for (sh, wi) in ((1, 1), (2, 0)):
                I("dve", "scalar_tensor_tensor", [(uek, "m"), (uek, "h"), "cw", yk], [yk], out=v3(yt[:, 0:NT_]), in0=uev(uet, sh, L + sh),
                  scalar=cw[:, c, wi:wi + 1], in1=v3(yt[:, 0:NT_]), op0=ALU.mult, op1=ALU.add)
            I("dve", "tensor_tensor", [bcbk, yk], [("ycb", c)], out=ycb[:, c, 0:NT_], in0=bcb[:, 0:NT_], in1=yt[:, 0:NT_], op=ALU.mult)
        attention(kind, j)
        for c in range(8):
            Wm, wmk = ws.next()
            Wg3 = Wm[:, 0:2048].rearrange("p (k c) -> p k c", k=8)
            Wa3 = Wm[0:64, 2048:3072].rearrange("p (h m) -> p h m", h=8)
            Wo3 = Wm[:, 3072:3584].rearrange("p (k m) -> p k m", k=4)
            (bga, bgak), (bgc, bgck), (bya, byak), (byc, byck) = [psb() for _ in range(4)]
            mm_acc(bga[:, 0:NT_], bgak, [(Wg3[:, kc, 0:128], hT[:, kc, 0:NT_], [("hT", kc)]) for kc in range(8)], [wmk])
            mm_acc(bgc[:, 0:NT_], bgck, [(Wg3[:, kc, 128:256], hT[:, kc, 0:NT_], [("hT", kc)]) for kc in range(8)], [wmk])
            mm_acc(bya[:, 0:NT_], byak, [(Wa3[:, h, :], attnT[0:64, h, 0:NT_], ["attnT"]) for h in range(8)], [wmk])
            mm_acc(byc[:, 0:NT_], byck, [(Wo3[:, kc, :], ycb[:, kc, 0:NT_], [("ycb", kc)]) for kc in range(4)], [wmk])
            i2 = c % 2
            I("act", "activation", [bgak], ["cxs%d" % i2], out=sga[i2][:, 0:NT_], in_=bga[:, 0:NT_], func=AF.Sigmoid)
            I("act", "activation", [bgck], ["yac%d" % i2], out=sgc[i2][:, 0:NT_], in_=bgc[:, 0:NT_], func=AF.Sigmoid)
            I("dve", "tensor_tensor", [byak, "cxs%d" % i2], ["t1_%d" % i2], out=t1[i2][:, 0:NT_], in0=bya[:, 0:NT_], in1=sga[i2][:, 0:NT_], op=ALU.mult)
            I("dve", "tensor_tensor", [byck, "yac%d" % i2], ["yac%d" % i2], out=sgc[i2][:, 0:NT_], in0=byc[:, 0:NT_], in1=sgc[i2][:, 0:NT_], op=ALU.mult)
            I("dve", "tensor_tensor", ["t1_%d" % i2, "yac%d" % i2], [("mT", c)], out=mT[:, c, 0:NT_], in0=t1[i2][:, 0:NT_], in1=sgc[i2][:, 0:NT_], op=ALU.add)
        for jc in range(2):
            Wo_, wok = ws.next()
            W3 = Wo_[:, :].rearrange("p (k c) -> p k c", k=8)
            for s in range(NS_):
                bank, bk = psb()
                mm_acc(bank[:, :], bk, [(mT[:, kc, s * 128:(s + 1) * 128], W3[:, kc, :], [("mT", kc)]) for kc in range(8)], [wok])
                xs_ = xr[:, s, jc * 512:(jc + 1) * 512]
                I("dve", "tensor_tensor", [bk, ("xr", s)], [("xr", s)], out=xs_, in0=xs_, in1=bank[:, :], op=ALU.add)
        norm_to_hT(NS_, NT_)
        for pu in range(8):
            Wu, wuk = ws.next()
            W3 = Wu[:, :].rearrange("p (k c) -> p k c", k=8)
            for m in range(4):
                bank, bk = psb()
                mm_acc(bank[:, 0:NT_], bk, [(W3[:, kc, m * 128:(m + 1) * 128], hT[:, kc, 0:NT_], [("hT", kc)]) for kc in range(8)], [wuk])
                i2 = m % 2
                I("act", "activation", [bk], ["t1_%d" % i2], out=t1[i2][:, 0:NT_], in_=bank[:, 0:NT_], func=AF.Relu)
                wr = [("fT", pu * 4 + m)] + (REGB + ["regB_tok"] if (pu == 0 and m == 0) else [])
                I("pool", "tensor_tensor", ["t1_%d" % i2], wr, out=fT[:, pu * 4 + m, 0:NT_], in0=t1[i2][:, 0:NT_], in1=t1[i2][:, 0:NT_], op=ALU.mult)
        for chh in range(2):
            bks = [psb() for _ in range(NS_)]
            for kq in range(4):
                Wd, wdk = ws.next()
                W3 = Wd[:, :].rearrange("p (k c) -> p k c", k=8)
                for s in range(NS_):
                    bank, bk = bks[s]
                    for kc in range(8):
                        I("pe", "matmul", [("fT", kq * 8 + kc), wdk], [bk], out=bank[:, :], lhsT=fT[:, kq * 8 + kc, s * 128:(s + 1) * 128], rhs=W3[:, kc, :],
                          start=(kq == 0 and kc == 0), stop=(kq == 3 and kc == 7))
            for s in range(NS_):
                bank, bk = bks[s]
                xs_ = xr[:, s, chh * 512:(chh + 1) * 512]
                I("dve", "tensor_tensor", [bk, ("xr", s)], [("xr", s)], out=xs_, in0=xs_, in1=bank[:, :], op=ALU.add)
        norm_to_hT(NS_, NT_)
        psrc = pown[j].rearrange("(s p) f -> p s f", p=128) if kind == "P" else psam.rearrange("(s p) f -> p s f", p=128)
        DM("sp", pst[:, 0:NS_, :], psrc, ["regB_tok"], ["pst"], "pst")
        I("dve", "tensor_copy", ["pst"], ["pbf"], out=pbf[:, 0:NS_, :], in_=pst[:, 0:NS_, :])
        pt, pk = ptb()
        for k2 in range(2):
            for s in range(NS_):
                I("pe", "transpose", ["pbf", "ident"], [pk], out=pt[:, k2 * 512 + s * 128: k2 * 512 + (s + 1) * 128], in_=pbf[:, s, k2 * 128:(k2 + 1) * 128], identity=ident[:])
        copy_op(evac_eng(), pT[:, 0:2, 0:NT_], pt[:, :].rearrange("p (k t) -> p k t", k=2)[:, :, 0:NT_], [pk], [("pT", 0), ("pT", 1)])
        for jc in range(2):
            Wg, wgk = ws.next()
            W3 = Wg[:, :].rearrange("p (k c) -> p k c", k=8)
            for s in range(NS_):
                bank, bk = psb()
                mm_acc(bank[:, :], bk, [(hT[:, kc, s * 128:(s + 1) * 128], W3[:, kc, :], [("hT", kc)]) for kc in range(8)], [wgk])
                I("act", "activation", [bk], [("gate", s)], out=gate[:, s, :], in_=bank[:, :], func=AF.Sigmoid)
            Wl, wlk = ws.next()
            Wl3 = Wl[:, 0:1024].rearrange("p (k c) -> p k c", k=2)
            for s in range(NS_):
                bank, bk = psb()
                mm_acc(bank[:, :], bk, [(pT[:, k2, s * 128:(s + 1) * 128], Wl3[:, k2, :], [("pT", k2)]) for k2 in range(2)], [wlk])
                I("dve", "tensor_tensor", [bk, ("gate", s)], [("gate", s)], out=gate[:, s, :], in0=bank[:, :], in1=gate[:, s, :], op=ALU.mult)
                xs_ = xr[:, s, jc * 512:(jc + 1) * 512]
                I("dve", "tensor_tensor", [("gate", s), ("xr", s)], [("xr", s)], out=xs_, in0=xs_, in1=gate[:, s, :], op=ALU.add)
        norm_stats(NS_)
        for s in range(NS_):
            i = s % 2
            I("dve", "scalar_tensor_tensor", [("xr", s), "rstd", "gfin"], ["stg%d" % i, ("stgq", 2 * i), ("stgq", 2 * i + 1)], out=stg[i][:, :], in0=xr[:, s, :], scalar=rstd[:, s:s + 1],
              in1=gfin[:, :], op0=ALU.mult, op1=ALU.mult)
            dst = y_own[j, s * 128:(s + 1) * 128, :] if kind == "P" else y_s[:, :]
            DM("sp", dst, stg[i][:, :], ["stg%d" % i], [], ("stg", i))
        I("pool", "memset", [], [("fT", i) for i in range(32)] + ["pst", "pbf", ("pT", 0), ("pT", 1)] + [("gate", s) for s in range(4)] + REGB,
          ap=cp[0][:, 0:1], constant=1.0)

    grp_pieces = [(PQ, 4096)] + [(PCONV + c, 3072) for c in range(4)] + [(PMIX + c, 3584) for c in range(8)] + \
                 [(PO, 4096), (PO + 1, 4096)] + [(PUP + i, 4096) for i in range(8)] + [(PDN + i, 4096) for i in range(8)] + \
                 [(PPG, 4096), (PPLE, 1024), (PPG + 1, 4096), (PPLE + 1, 1024)]
    for _ in range(17):
        ws.plan([(PK, 4096), (PV, 4096)])
    for _ in range(9):
        ws.plan(grp_pieces)

    prep_unscaled()
    build_prep_items()
    pi = [0]

    def run_prep(n):
        for _ in range(n):
            if pi[0] < len(prep_items):
                prep_items[pi[0]]()
                pi[0] += 1
    STAGE = _STAGE[0]
    run_prep(16)
    if STAGE >= 2:
        for g in range(NG if _SUB[0] == 0 else _NGRP[0]):
            phase1a("P", g)
            run_prep(6)
        if _SUB[0] in (0, 8):
            phase1a("S", 0)
    run_prep(len(prep_items))
    if STAGE >= 3:
        cache_prep()
    I("pool", "memset", [], ["wst0", "wst1", "wbf0", "wbf1", "ckb0", "ckb1"] + [("xr2", s_) for s_ in range(4)] + REGB, ap=cp[0][:, 0:1], constant=1.0)
    if STAGE >= 4:
        for j in range(8 if STAGE >= 5 else 1):
            group("P", j)
    if STAGE >= 6:
        group("S", 0)
    P.emit()
    print("instructions recorded:", len(P.ins))
    return nc


def _fix_none_keys():
    pass


_NC = [None]


def kernel(x_prompt, x_sample, p_prompt, p_sample, cache_k, cache_v, cache_conv,
           g_mix, w_in, conv_w, w_attn_out, w_conv_out, w_o,
           g_ffn, w_up, w_down, g_ple, w_ple_gate, w_ple, g_final):
    f = lambda a: np.ascontiguousarray(np.asarray(a, dtype=np.float32))
    x_prompt, x_sample, p_prompt, p_sample = f(x_prompt), f(x_sample), f(p_prompt), f(p_sample)
    cache_k, cache_v, cache_conv = f(cache_k), f(cache_v), f(cache_conv)
    if _NC[0] is None:
        _NC[0] = build_program()
    nc = _NC[0]
    diag = np.zeros((4, 128, 512), np.float32)
    for r in range(4):
        diag[r] = (np.arange(512)[None, :] <= (128 * r + np.arange(128))[:, None]).astype(np.float32)
    ones = np.ones((128, 512), np.float32)
    zeros = np.zeros((128, 512), np.float32)
    gvec = np.stack([np.asarray(g, np.float32).reshape(8, 128).T for g in (g_mix[0], g_ffn[0], g_ple[0])], axis=1).reshape(128, 24)
    shared = dict(
        gfin=f(np.broadcast_to(np.asarray(g_final, np.float32), (128, D))),
        convwT=f(np.asarray(conv_w[0], np.float32).T),
        gvec=f(gvec),
        w_in=f(w_in[0]), w_ao=f(w_attn_out[0]), w_co=f(w_conv_out[0]), w_o=f(w_o[0]),
        w_up=f(w_up[0]), w_dn=f(w_down[0]), w_pg=f(w_ple_gate[0]), w_ple=f(w_ple[0]),
    )
    in_maps = []
    for c in range(8):
        b, p = c // 2, c % 2
        xs = x_prompt[b, ::-1]
        ps = p_prompt[0, b, ::-1]
        xown = np.stack([xs[512 * g:512 * g + 512] for g in GRP[p]])
        pown = np.stack([ps[512 * g:512 * g + 512] for g in GRP[p]])
        xhalo = np.zeros((8, 2, D), np.float32)
        for j, g in enumerate(GRP[p]):
            if g < 15:
                xhalo[j] = xs[512 * (g + 1): 512 * (g + 1) + 2]
        mbank = np.zeros((17, 128, 512), np.float32)
        for jpar in range(2):
            caseA = (p == 0 and jpar == 0) or (p == 1 and jpar == 1)
            for r in range(4):
                mbank[(jpar * 2 + 0) * 4 + r] = diag[r] if caseA else ones
                mbank[(jpar * 2 + 1) * 4 + r] = zeros if caseA else diag[r]
        mbank[16] = diag[0]
        sl = slice(4 * c, 4 * c + 4)
        xsam = x_sample[sl, ::-1].reshape(128, D)
        psam = p_sample[0, sl, ::-1].reshape(128, 256)
        ckc = cache_k[0, sl, ::-1].reshape(4, PAST, 512)
        cvc = cache_v[0, sl, ::-1].reshape(4, PAST, 512)
        cc = cache_conv[0, sl, ::-1]
        cconvT = np.transpose(cc, (2, 0, 1)).reshape(512, 8)
        m = dict(xall=f(xs), xown=f(xown), xhalo=f(xhalo), pown=f(pown), xsam=f(xsam), psam=f(psam),
                 ck=f(ckc), cv=f(cvc), cconvT=f(cconvT), mb=f(mbank))
        m.update(shared)
        in_maps.append(m)
    ncr = _NCORES[0]
    res = run_bass_kernel_spmd(nc, in_maps[:ncr], core_ids=list(range(ncr)))
    R = list(res.results) + [res.results[0]] * (8 - ncr)
    y_prompt = np.zeros((4, T, D), np.float32)
    k_prompt = np.zeros((1, 4, T, 8, 64), np.float32)
    v_prompt = np.zeros((1, 4, T, 8, 64), np.float32)
    conv_prompt = np.zeros((1, 4, 2, 512), np.float32)
    y_sample = np.zeros((32, 32, D), np.float32)
    k_sample = np.zeros((1, 32, 32, 8, 64), np.float32)
    v_sample = np.zeros((1, 32, 32, 8, 64), np.float32)
    conv_sample = np.zeros((1, 32, 2, 512), np.float32)
    for c in range(8):
        b, p = c // 2, c % 2
        r = R[c]
        ys = np.zeros((T, D), np.float32) if p == 0 else None
        for j, g in enumerate(GRP[p]):
            y_prompt[b, T - 512 * (g + 1): T - 512 * g] = np.asarray(r["y_own"][j])[::-1]
        if p == 0:
            k_prompt[0, b] = np.asarray(r["k_all"])[::-1].reshape(T, 8, 64)
            v_prompt[0, b] = np.asarray(r["v_all"])[::-1].reshape(T, 8, 64)
            cp_ = np.asarray(r["conv_p"])
            u = np.transpose(cp_, (1, 0, 2)).reshape(512, 2)
            conv_prompt[0, b, 0] = u[:, 1]
            conv_prompt[0, b, 1] = u[:, 0]
        sl = slice(4 * c, 4 * c + 4)
        y_sample[sl] = np.asarray(r["y_s"]).reshape(4, 32, D)[:, ::-1]
        k_sample[0, sl] = np.asarray(r["k_s"]).reshape(4, 32, 8, 64)[:, ::-1]
        v_sample[0, sl] = np.asarray(r["v_s"]).reshape(4, 32, 8, 64)[:, ::-1]
        cs = np.asarray(r["conv_s"])
        u = np.transpose(cs, (2, 1, 0, 3)).reshape(4, 512, 2)
        conv_sample[0, sl, 0] = u[:, :, 1]
        conv_sample[0, sl, 1] = u[:, :, 0]
    return (y_prompt, y_sample, k_prompt, v_prompt, conv_prompt, k_sample, v_sample, conv_sample)
```

```python
import numpy as np
import concourse.bass as bass
import concourse.mybir as mybir
from concourse.bass_utils import run_bass_kernel_spmd

F32 = mybir.dt.float32
BF16 = mybir.dt.bfloat16
AF = mybir.ActivationFunctionType
ALU = mybir.AluOpType

D = 1024
NPROJ = 5120
T = 8192
NG = 16
GRP = {0: [0, 3, 4, 7, 8, 11, 12, 15], 1: [1, 2, 5, 6, 9, 10, 13, 14]}
EPS = 1e-6
PAST = 2048

PK, PV, PQ, PCONV, PMIX, PO, PUP, PDN, PPG, PPLE, NPIECE = 0, 1, 2, 3, 7, 15, 17, 25, 33, 35, 37


_STAGE = [9]
_SUB = [0]
_NGRP = [1]
_NCORES = [8]


class Prog:
    def __init__(self, nc):
        self.nc = nc
        self.ins = []
        self.lastw = {}
        self.reads = {}

    def _add(self, eng, fn, reads, writes, dma, semkey):
        idx = len(self.ins)
        deps = set()
        for b in reads:
            if b in self.lastw:
                deps.add(self.lastw[b])
        for b in writes:
            if b in self.lastw:
                deps.add(self.lastw[b])
            for r in self.reads.get(b, ()):
                deps.add(r)
        self.ins.append(dict(eng=eng, fn=fn, deps=deps, dma=dma, semkey=semkey))
        for b in reads:
            self.reads.setdefault(b, []).append(idx)
        for b in writes:
            self.lastw[b] = idx
            self.reads[b] = []
        return idx

    def op(self, eng, fn, reads=(), writes=()):
        return self._add(eng, fn, reads, writes, False, None)

    def dma(self, eng, fn, reads=(), writes=(), semkey=None):
        assert semkey is not None
        return self._add(eng, fn, reads, writes, True, semkey)

    def emit(self):
        nc = self.nc
        engs = dict(pe=nc.tensor, act=nc.scalar, dve=nc.vector, pool=nc.gpsimd, sp=nc.sync)
        ins = self.ins
        needed = set()
        for i, it in enumerate(ins):
            for d in it["deps"]:
                pd = ins[d]
                if pd["dma"] or pd["eng"] != it["eng"] or it["dma"] or pd["eng"] != "pe":
                    needed.add(d)
        sems = {}

        def getsem(k):
            if k not in sems:
                sems[k] = nc.alloc_semaphore("s%d" % len(sems))
            return sems[k]
        cnt = {}
        comp = {}
        waited = {}
        last_dma = {}
        gtot = {}
        for it in ins:
            if it["dma"] and isinstance(it["semkey"], tuple) and it["semkey"][0] == "grp":
                gtot[it["semkey"]] = gtot.get(it["semkey"], 0) + 16
        for i, it in enumerate(ins):
            eng = it["eng"]
            e = engs[eng]
            best = {}
            for d in it["deps"]:
                if d not in comp:
                    continue
                sk, val = comp[d]
                if sk == ("e", "pe") and eng == "pe" and not it["dma"]:
                    continue
                if best.get(sk, 0) < val:
                    best[sk] = val
            for sk, val in best.items():
                if waited.get((eng, sk), 0) >= val:
                    continue
                e.wait_ge(getsem(sk), val)
                waited[(eng, sk)] = val
            r = it["fn"](e)
            if it["dma"]:
                sk = ("d", it["semkey"])
                cnt[sk] = cnt.get(sk, 0) + 16
                r.then_inc(getsem(sk), 16)
                comp[i] = (sk, gtot.get(it["semkey"], cnt[sk]))
                last_dma[sk] = cnt[sk]
            elif i in needed:
                sk = ("e", eng)
                cnt[sk] = cnt.get(sk, 0) + 1
                r.then_inc(getsem(sk), 1)
                comp[i] = (sk, cnt[sk])
        for sk, val in last_dma.items():
            if waited.get(("sp", sk), 0) < val:
                nc.sync.wait_ge(getsem(sk), val)


def build_program():
    nc = bass.Bass("TRN2", target_bir_lowering=False)
    P = Prog(nc)

    def I(eng, meth, reads, writes, **kw):
        P.op(eng, lambda e: getattr(e, meth)(**kw), reads=reads, writes=writes)

    def DM(q, out, in_, reads, writes, semkey):
        P.dma(q, lambda e: e.dma_start(out=out, in_=in_), reads=reads, writes=writes, semkey=semkey)

    def din(name, shape, dt=F32):
        return nc.dram_tensor(name, list(shape), dt, kind="ExternalInput").ap()

    def dout(name, shape, dt=F32):
        return nc.dram_tensor(name, list(shape), dt, kind="ExternalOutput").ap()

    def dscr(name, shape, dt=BF16):
        return nc.dram_tensor(name, list(shape), dt, kind="Internal").ap()

    xall = din("xall", [T, D])
    xown = din("xown", [8, 512, D])
    xhalo = din("xhalo", [8, 2, D])
    pown = din("pown", [8, 512, 256])
    xsam = din("xsam", [128, D])
    psam = din("psam", [128, 256])
    ck = din("ck", [4, PAST, 512])
    cv = din("cv", [4, PAST, 512])
    cconvT = din("cconvT", [512, 8])
    mb = din("mb", [17, 128, 512])
    gfin_d = din("gfin", [128, D])
    convwT = din("convwT", [512, 3])
    gvec_d = din("gvec", [128, 24])
    w_in = din("w_in", [D, NPROJ])
    w_ao = din("w_ao", [512, D])
    w_co = din("w_co", [512, D])
    w_o = din("w_o", [D, D])
    w_up = din("w_up", [D, 4096])
    w_dn = din("w_dn", [4096, D])
    w_pg = din("w_pg", [D, D])
    w_ple = din("w_ple", [256, D])

    y_own = dout("y_own", [8, 512, D])
    k_all = dout("k_all", [T, 512])
    v_all = dout("v_all", [T, 512])
    conv_p = dout("conv_p", [128, 4, 2])
    y_s = dout("y_s", [128, D])
    k_s = dout("k_s", [128, 512])
    v_s = dout("v_s", [128, 512])
    conv_s = dout("conv_s", [128, 4, 4, 2])

    wimg = dscr("wimg", [NPIECE, 128, 4096])
    KTp = dscr("KTp", [8, 64, T])
    Vp = dscr("Vp", [8, 4, 128, 1024])
    KTs = dscr("KTs", [4, 8, 64, 32 + PAST])
    Vs = dscr("Vs", [4, PAST, 512])
    Vsn = dscr("Vsn", [4, 8, 32, 64])
    Vq = dscr("Vq", [8, 4, 128, 1024])
    zrow = dscr("zrow", [1, 512])

    globals_hT = [None]
    globals_xr = [None]
    x_issued = set()
    SB_BASE = 16512
    off = [SB_BASE]

    def sb(name, shape, dt, at=None):
        nbytes = int(np.prod(shape[1:])) * (4 if dt == F32 else 2)
        nbytes = (nbytes + 31) // 32 * 32
        if at is None:
            o = off[0]
            off[0] += nbytes
        else:
            o = at
        assert o + nbytes <= SB_BASE + 212800, (name, o, nbytes)
        return nc.alloc_sbuf_tensor_at(name, list(shape), dt, offset=o)

    ident = sb("ident", [128, 128], BF16)
    zmask = sb("zmask", [128, 512], BF16)
    mbt = sb("mbt", [128, 17, 512], BF16)
    gfin = sb("gfin_s", [128, D], F32)
    cw = sb("cw", [128, 4, 3], F32)
    gvec = sb("gvec_s", [128, 24], F32)
    epsb = sb("epsb", [128, 1], F32)
    ss = sb("ss", [128, 4], F32)
    sq = sb("sq", [128, 4], F32)
    rstd = sb("rstd", [128, 4], F32)
    ssh = sb("ssh", [128, 1], F32)
    sqh = sb("sqh", [128, 1], F32)
    rsh = sb("rsh", [128, 1], F32)
    xr = sb("xr", [128, 4, D], F32)
    hbf = sb("hbf", [128, 4, D], BF16)
    hT = sb("hT", [128, 8, 512], BF16)
    qT = sb("qT", [64, 8, 512], BF16)
    attnT = sb("attnT", [64, 8, 512], BF16)
    ycb = sb("ycb", [128, 4, 512], BF16)
    mT = sb("mT", [128, 8, 512], BF16)
    xh = sb("xh", [2, D], F32)
    hhb = sb("hhb", [2, D], BF16)
    hTh = sb("hTh", [128, 16], BF16)
    cxh = sb("cxh", [128, 2], F32)
    globals_hT[0] = hT
    globals_xr[0] = xr
    wring = [sb("wring%d" % i, [128, 4096], BF16) for i in range(4)]
    cxs = [sb("cxs%d" % i, [128, 512], F32) for i in range(2)]
    ue = [sb("ue%d" % i, [128, 520], F32) for i in range(2)]
    yac = [sb("yac%d" % i, [128, 512], F32) for i in range(2)]
    sga = [cxs[0], cxs[1]]
    sgc = [yac[0], yac[1]]
    t1 = [sb("t1_%d" % i, [128, 512], F32) for i in range(2)]
    stg = [sb("stg%d" % i, [128, D], F32) for i in range(2)]
    junk = sb("junk", [128, D], BF16)
    regB = off[0]
    KTr = [sb("KTr%d" % i, [64, 2048], BF16) for i in range(2)]
    Vr = [sb("Vr%d" % i, [128, 1024], BF16) for i in range(2)]
    Vd = [sb("Vd%d" % i, [128, 1024], BF16) for i in range(2)]
    om = [sb("om%d" % i, [128, 512], F32) for i in range(4)]
    cp = [sb("cp%d" % i, [128, 1026], F32) for i in range(4)]
    Ab = [sb("Ab%d" % i, [128, 528], BF16) for i in range(8)]
    ATb = [sb("ATb%d" % i, [128, 1024], BF16) for i in range(4)]
    KTn = sb("KTn", [64, 32], BF16)
    Vn = sb("Vn", [32, 64], BF16)
    endB = off[0]
    fT = sb("fT", [128, 32, 512], BF16, at=regB)
    gate = sb("gate", [128, 4, 512], F32, at=regB + 32768)
    pst = sb("pst", [128, 4, 256], F32, at=regB + 32768 + 8192)
    pbf = sb("pbf", [128, 4, 256], BF16, at=regB + 32768 + 12288)
    pT = sb("pT", [128, 2, 512], BF16, at=regB + 32768 + 14336)
    assert regB + 32768 + 16384 <= endB
    assert regB + 32768 <= endB, (regB, endB)
    wst = [sb("wst%d" % i, [128, D], F32, at=regB + i * 4096) for i in range(2)]
    wbf = [sb("wbf%d" % i, [128, D], BF16, at=regB + 8192 + i * 2048) for i in range(2)]
    ckb = [sb("ckb%d" % i, [128, 4, 512], BF16, at=regB + 12288 + i * 4096) for i in range(2)]
    xr2 = sb("xr2", [128, 4, D], F32, at=regB + 20480)
    assert regB + 20480 + 16384 <= endB
    REGB = ["KTr0", "KTr1", "Vr0", "Vr1", "Vd0", "Vd1"] + ["om%d" % i for i in range(4)] + \
           ["Ab%d" % i for i in range(8)] + ["ATb%d" % i for i in range(4)] + ["KTn", "Vn"]
    for i in range(4):
        REGB += [("cp%d" % i, "c", 0), ("cp%d" % i, "s", 0), ("cp%d" % i, "s", 1)]
    print("SBUF used", off[0])

    PS = [nc.alloc_psum_tensor("ps%d" % i, [128, 512], F32) for i in range(6)]
    PT = [nc.alloc_psum_tensor("pt%d" % i, [128, 1024], BF16) for i in range(2)]
    rr_ = dict(ps=0, pt=0, ev=0, at=0, kv=0, pr=0, om=0, sg=0)

    def psb():
        i = rr_["ps"] % 6
        rr_["ps"] += 1
        return PS[i], "ps%d" % i

    def ptb():
        i = rr_["pt"] % 2
        rr_["pt"] += 1
        return PT[i], ("pt", i)

    def evac_eng():
        rr_["ev"] += 1
        return "act" if rr_["ev"] % 2 else "dve"

    def copy_op(eng, out, in_, reads, writes):
        if eng == "act":
            I("act", "copy", reads, writes, out=out, in_=in_)
        else:
            I(eng, "tensor_copy", reads, writes, out=out, in_=in_)

    I("pool", "memset", [], ["ident"], ap=ident[:], constant=0.0)
    I("pool", "affine_select", ["ident"], ["ident"], out=ident[:], in_=ident[:], pattern=[[-1, 128]],
      compare_op=ALU.not_equal, fill=1.0, base=0, channel_multiplier=1)
    I("pool", "memset", [], ["zmask"], ap=zmask[:], constant=0.0)
    I("pool", "memset", [], ["epsb"], ap=epsb[:], constant=EPS)
    DM("pool", mbt[:], mb.rearrange("m p k -> p m k"), [], ["mbt"], "mbt")
    DM("sp", gfin[:], gfin_d, [], ["gfin"], "gfin")
    DM("sp", cw[:], convwT.rearrange("(c p) j -> p c j", p=128), [], ["cw"], "cw")
    DM("sp", gvec[:], gvec_d, [], ["gvec"], "gvec")
    vq_keys = {rb: [] for rb in range(4)}
    DM("sp", zrow, zmask[0:1, 0:512], ["zmask"], ["zrow"], "zrow")

    parts = {}

    def wpart(piece, part):
        parts.setdefault(piece, set()).add(part)
        return ("wimg", piece, part)

    def img(piece, kc, ncol, c0, w):
        return wimg[piece, :, kc * ncol + c0: kc * ncol + c0 + w]

    def prep_unscaled():
        for j in range(2):
            src = w_o.rearrange("(k p) (j c) -> j p k c", p=128, j=2)[j]
            dst = wimg[PO + j].rearrange("p (k c) -> p k c", k=8)
            DM("pool", dst, src, [], [wpart(PO + j, 0)], ("grp", "pu"))
        for ch in range(2):
            for kq in range(4):
                pc = PDN + ch * 4 + kq
                src = w_dn[kq * 1024:(kq + 1) * 1024, ch * 512:(ch + 1) * 512].rearrange("(k p) c -> p k c", p=128)
                dst = wimg[pc].rearrange("p (k c) -> p k c", k=8)
                DM("pool", dst, src, [], [wpart(pc, 0)], ("grp", "pu"))
        for j in range(2):
            src = w_ple[:, j * 512:(j + 1) * 512].rearrange("(k p) c -> p k c", p=128)
            dst = wimg[PPLE + j, :, 0:1024].rearrange("p (k c) -> p k c", k=2)
            DM("pool", dst, src, [], [wpart(PPLE + j, 0)], ("grp", "pu"))
        for c in range(8):
            src = w_ao[:, c * 128:(c + 1) * 128].rearrange("(h q) m -> q h m", q=64)
            dst = wimg[PMIX + c, 0:64, 2048:3072].rearrange("q (h m) -> q h m", h=8)
            DM("pool", dst, src, [], [wpart(PMIX + c, "ao")], ("grp", "pu"))
            dst2 = wimg[PMIX + c, 64:128, 2048:3072].rearrange("q (h m) -> q h m", h=8)
            DM("pool", dst2, src, [], [wpart(PMIX + c, "ao2")], ("grp", "pu"))
            src = w_co[:, c * 128:(c + 1) * 128].rearrange("(k p) m -> p k m", p=128)
            dst = wimg[PMIX + c, :, 3072:3584].rearrange("p (k m) -> p k m", k=4)
            DM("pool", dst, src, [], [wpart(PMIX + c, "co")], ("grp", "pu"))

    prep_items = []
    prep_loaded = set()
    multi_keys = {"conv": [], "mix": []}

    prep_specs = []

    def scaled_item(W, kc, c0, gsel, stores):
        idx = len(prep_specs)
        prep_specs.append((W, kc, c0))

        def issue_load(k):
            if k >= len(prep_specs) or k in prep_loaded:
                return
            prep_loaded.add(k)
            W_, kc_, c0_ = prep_specs[k]
            i_ = k % 2
            DM("pool", wst[i_][:], W_[kc_ * 128:(kc_ + 1) * 128, c0_:c0_ + 1024], [], ["wst%d" % i_], ("wst", i_))

        def run():
            i = idx % 2
            issue_load(idx)
            issue_load(idx + 1)
            I("act", "activation", ["wst%d" % i, "gvec"], ["wbf%d" % i], out=wbf[i][:], in_=wst[i][:], func=AF.Copy,
              scale=gvec[:, gsel * 8 + kc: gsel * 8 + kc + 1])
            for n_, (sl, dst, pk) in enumerate(stores):
                DM("pool", dst, sl(wbf[i]), ["wbf%d" % i], [pk], ("wbf", i, n_))
        prep_items.append(run)

    def mkey(kind, kc, t_):
        k = ("wimg", kind, kc, t_)
        multi_keys[kind].append(k)
        return k

    def build_prep_items():
        for kc in range(8):
            scaled_item(w_in, kc, 0, 0, [
                (lambda t: t[:, 512:1024], img(PK, kc, 512, 0, 512), wpart(PK, kc)),
                (lambda t: t[:, 0:512], img(PQ, kc, 512, 0, 512), wpart(PQ, kc)),
            ])
            scaled_item(w_in, kc, 1024, 0, [
                (lambda t: t[:, 0:512], img(PV, kc, 512, 0, 512), wpart(PV, kc)),
                (lambda t: t[:, 512:1024].rearrange("p (c i) -> p c i", c=4),
                 wimg[PCONV:PCONV + 4, :, kc * 384: kc * 384 + 128].rearrange("c p i -> p c i"), mkey("conv", kc, 0)),
            ])
        for kc in range(8):
            scaled_item(w_in, kc, 2048, 0, [
                (lambda t: t[:, 0:512].rearrange("p (c i) -> p c i", c=4),
                 wimg[PCONV:PCONV + 4, :, kc * 384 + 128: kc * 384 + 256].rearrange("c p i -> p c i"), mkey("conv", kc, 1)),
                (lambda t: t[:, 512:1024].rearrange("p (c i) -> p c i", c=4),
                 wimg[PCONV:PCONV + 4, :, kc * 384 + 256: kc * 384 + 384].rearrange("c p i -> p c i"), mkey("conv", kc, 2)),
            ])
        for kc in range(8):
            for t_ in range(2):
                scaled_item(w_in, kc, 3072 + t_ * 1024, 0, [
                    (lambda t: t[:, :].rearrange("p (c i) -> p c i", c=8),
                     wimg[PMIX:PMIX + 8, :, kc * 256 + t_ * 128: kc * 256 + t_ * 128 + 128].rearrange("c p i -> p c i"), mkey("mix", kc, t_)),
                ])
        for kc in range(8):
            for s_ in range(4):
                scaled_item(w_up, kc, s_ * 1024, 1, [
                    (lambda t: t[:, 0:512], img(PUP + 2 * s_, kc, 512, 0, 512), wpart(PUP + 2 * s_, kc)),
                    (lambda t: t[:, 512:1024], img(PUP + 2 * s_ + 1, kc, 512, 0, 512), wpart(PUP + 2 * s_ + 1, kc)),
                ])
        for kc in range(8):
            scaled_item(w_pg, kc, 0, 2, [
                (lambda t: t[:, 0:512], img(PPG, kc, 512, 0, 512), wpart(PPG, kc)),
                (lambda t: t[:, 512:1024], img(PPG + 1, kc, 512, 0, 512), wpart(PPG + 1, kc)),
            ])

    class WS:
        def __init__(self):
            self.seq = []
            self.issued = 0
            self.used = 0

        def plan(self, pieces):
            self.seq.extend(pieces)

        def _issue(self):
            k = self.issued
            piece, ncol = self.seq[k]
            slot = k % 4
            rd = [("wimg", piece, p) for p in parts.get(piece, ())]
            if PCONV <= piece < PCONV + 4:
                rd += multi_keys["conv"]
            if PMIX <= piece < PMIX + 8:
                rd += multi_keys["mix"]
            DM("sp", wring[slot][:, 0:ncol], wimg[piece, :, 0:ncol], rd, ["wring%d" % slot], ("wring", slot))
            self.issued += 1

        def next(self):
            k = self.used
            while self.issued < min(len(self.seq), k + 4):
                self._issue()
            self.used += 1
            slot = k % 4
            return wring[slot], "wring%d" % slot

    ws = WS()

    def norm_stats(NS_, xr=xr, xk="xr"):
        for s in range(NS_):
            I("act", "activation", [(xk, s)], ["junk", "ss"], out=junk[:], in_=xr[:, s, :], func=AF.Square, accum_out=ss[:, s:s + 1])
        I("act", "activation", ["ss", "epsb"], ["sq"], out=sq[:, 0:NS_], in_=ss[:, 0:NS_], func=AF.Sqrt, bias=epsb[:, 0:1], scale=1.0 / D)
        I("dve", "reciprocal", ["sq"], ["rstd"], out=rstd[:, 0:NS_], in_=sq[:, 0:NS_])

    def norm_to_hT(NS_, NT_, hT=hT, hk="hT", xr=xr, xk="xr"):
        norm_stats(NS_, xr, xk)
        for s in range(NS_):
            I("dve", "tensor_scalar", [(xk, s), "rstd"], [("hbf", s)], out=hbf[:, s, :], in0=xr[:, s, :], scalar1=rstd[:, s:s + 1], scalar2=None, op0=ALU.mult)
        import os
        npair = int(os.environ.get("DBG_NPAIR", "4"))
        dbg_ev = os.environ.get("DBG_EV", "")
        for kp in range(npair):
            pt, pk = ptb()
            for kk in range(2):
                kc = kp * 2 + kk
                for s in range(NS_):
                    I("pe", "transpose", [("hbf", s), "ident"], [pk], out=pt[:, kk * 512 + s * 128: kk * 512 + (s + 1) * 128],
                      in_=hbf[:, s, kc * 128:(kc + 1) * 128], identity=ident[:])
            if dbg_ev == "none":
                continue
            copy_op(dbg_ev or evac_eng(), hT[:, kp * 2:kp * 2 + 2, 0:NT_], pt[:, :].rearrange("p (k t) -> p k t", k=2)[:, :, 0:NT_], [pk],
                    [(hk, kp * 2), (hk, kp * 2 + 1)])

    def mm_acc(bank_ap, bk, pairs, extra_reads):
        n = len(pairs)
        for i, (l, r, rk) in enumerate(pairs):
            I("pe", "matmul", list(rk) + list(extra_reads), [bk], out=bank_ap, lhsT=l, rhs=r, start=(i == 0), stop=(i == n - 1))

    def phase1a(kind, g):
        NT_ = 512 if kind == "P" else 128
        NS_ = NT_ // 128
        hT, hk = (globals_hT[0], "hT") if g % 2 == 0 else (mT, "mT")

        def stgr():
            i = rr_["sg"] % 4
            rr_["sg"] += 1
            return stg[i // 2][:, (i % 2) * 512:(i % 2 + 1) * 512], ("stgq", i)
        def xload(kind_, g_):
            xt, xk_ = (globals_xr[0], "xr") if (g_ % 2 == 0) else (xr2, "xr2")
            if (kind_, g_) in x_issued:
                return xt, xk_
            x_issued.add((kind_, g_))
            for s in range(4 if kind_ == "P" else 1):
                src = xall[g_ * 512 + s * 128: g_ * 512 + (s + 1) * 128, :] if kind_ == "P" else xsam[:, :]
                DM("sp", xt[:, s, :], src, [], [(xk_, s)], (xk_, s))
            return xt, xk_
        gi = g if kind == "P" else 16
        xr, xk = xload(kind, gi)
        if kind == "P":
            if g + 1 < NG:
                xload("P", g + 1)
            else:
                xload("S", 16)
        SUB = _SUB[0]
        import os
        dbs = int(os.environ.get("DBG_S", "9")) if kind == "S" else 9
        if SUB == 1:
            norm_stats(NS_, xr, xk)
            return
        norm_to_hT(NS_, NT_, hT, hk, xr, xk)
        if SUB == 2 or dbs == 1:
            return
        Wk, wkk = ws.next()
        Wk3 = Wk[:, :].rearrange("p (k c) -> p k c", k=8)
        if SUB == 3:
            return
        for h in range(8):
            bank, bk = psb()
            mm_acc(bank[0:64, 0:NT_], bk, [(Wk3[:, kc, h * 64:(h + 1) * 64], hT[:, kc, 0:NT_], [(hk, kc)]) for kc in range(8)], [wkk])
            copy_op(evac_eng(), attnT[0:64, h, 0:NT_], bank[0:64, 0:NT_], [bk], ["attnT"])
        if SUB == 4 or dbs == 2:
            return
        if kind == "P":
            DM("sp", KTp[:, :, g * 512:(g + 1) * 512].rearrange("h d t -> d h t"), attnT[0:64, :, :], ["attnT"], [("KTp", g)], "ktst")
        else:
            for i in range(4):
                DM("sp", KTs[i, :, :, 0:32].rearrange("h d t -> d h t"), attnT[0:64, :, i * 32:(i + 1) * 32], ["attnT"], [("KTs", i, "n")], ("ktst", i))
        if SUB == 5 or dbs == 3:
            return
        for s in range(NS_):
            bank, bk = psb()
            mm_acc(bank[:, :], bk, [(hT[:, kc, s * 128:(s + 1) * 128], Wk3[:, kc, :], [(hk, kc)]) for kc in range(8)], [wkk])
            sgt, sgk = stgr()
            copy_op(evac_eng(), sgt, bank[:, :], [bk], [sgk])
            dst = k_all[g * 512 + s * 128: g * 512 + (s + 1) * 128, :] if kind == "P" else k_s[:, :]
            DM("sp", dst, sgt, [sgk], [], sgk)
        if SUB == 6 or dbs == 4:
            return
        Wv, wvk = ws.next()
        Wv3 = Wv[:, :].rearrange("p (k c) -> p k c", k=8)
        for s in range(NS_):
            bank, bk = psb()
            mm_acc(bank[:, :], bk, [(hT[:, kc, s * 128:(s + 1) * 128], Wv3[:, kc, :], [(hk, kc)]) for kc in range(8)], [wvk])
            sgt, sgk = stgr()
            copy_op("act", sgt, bank[:, :], [bk], [sgk])
            copy_op("dve", ycb[:, s, :], sgt, [sgk], [("ycb", s)])
            dst = v_all[g * 512 + s * 128: g * 512 + (s + 1) * 128, :] if kind == "P" else v_s[:, :]
            DM("sp", dst, sgt, [sgk], [], sgk)
            if dbs == 5:
                continue
            if kind == "P":
                c = (g % 4) * 4 + s
                dstv = Vp[:, g // 4, :, c * 64:(c + 1) * 64].rearrange("h p d -> p h d")
                DM("sp", dstv, ycb[:, s, :].rearrange("p (h d) -> p h d", h=8), [("ycb", s)], [("Vp", g // 4, g % 4, s)], ("vst", s))
            else:
                for i in range(4):
                    DM("sp", Vsn[i].rearrange("h t d -> t h d"), ycb[i * 32:(i + 1) * 32, 0, :].rearrange("p (h d) -> p h d", h=8),
                       [("ycb", 0)], [("Vsn", i)], ("vsn", i))

    def cache_prep():
        for i in range(4):
            for hf in range(2):
                DM("pool", Vs[i, hf * 1024:(hf + 1) * 1024, :], cv[i, hf * 1024:(hf + 1) * 1024, :], [], [("Vs", i, hf)], ("grp", "cvs"))
        n = 0
        for i in range(4):
            for blk in range(4):
                b = n % 2
                n += 1
                DM("pool", ckb[b][:], ck[i, blk * 512:(blk + 1) * 512, :].rearrange("(c p) f -> p c f", p=128), [], ["ckb%d" % b], ("ckb", b))
                for hp in range(4):
                    pt, pk = ptb()
                    for hh in range(2):
                        h = hp * 2 + hh
                        for c in range(4):
                            I("pe", "transpose", ["ckb%d" % b, "ident"], [pk], out=pt[0:64, hh * 512 + c * 128: hh * 512 + (c + 1) * 128],
                              in_=ckb[b][:, c, h * 64:(h + 1) * 64], identity=ident[:])
                    copy_op(evac_eng(), attnT[0:64, hp * 2:hp * 2 + 2, :], pt[0:64, :].rearrange("p (k t) -> p k t", k=2), [pk], ["attnT"])
                DM("sp", KTs[i, :, :, 32 + blk * 512: 32 + (blk + 1) * 512].rearrange("h d t -> d h t"), attnT[0:64, :, :],
                   ["attnT"], [("KTs", i, blk)], "ktst")

    def attention(kind, j):
        chains = []
        if kind == "P":
            kb0 = 2 * j
            jpar = j % 2
            for h in range(8):
                blocks = []
                for i, kb in enumerate(range(kb0, 16)):
                    if i == 0:
                        mk = [mbt[:, (jpar * 2 + 0) * 4 + r, :] for r in range(4)]
                    elif i == 1:
                        mk = [mbt[:, (jpar * 2 + 1) * 4 + r, :] for r in range(4)]
                    else:
                        mk = [zmask[:, :]] * 4
                    blocks.append(dict(nk=512, src=("P", h, kb // 4), sub=kb % 4, mask=mk))
                chains.append(dict(h=h, q0=0, R=4, nq=128, blocks=blocks))
        else:
            for i in range(4):
                for h in range(8):
                    blocks = [dict(nk=32, src=("N", i, h), sub=0, mask=[mbt[0:32, 16, 0:32]])]
                    for blk in range(4):
                        blocks.append(dict(nk=512, src=("S", i, h), sub=blk, mask=[zmask[0:32, :]]))
                    chains.append(dict(h=h, q0=i * 32, R=1, nq=32, blocks=blocks))
        steps = [(ci, bi) for ci, ch in enumerate(chains) for bi in range(len(ch["blocks"]))]
        cur_src = [None, None]
        pending = None
        src_order = []
        for ch in chains:
            for blk in ch["blocks"]:
                if blk["src"][0] != "N" and blk["src"] not in src_order:
                    src_order.append(blk["src"])
        kv_slot = {}

        def issue_kv(src):
            slot = rr_["kv"] % 2
            rr_["kv"] += 1
            kv_slot[src] = slot
            V3d = Vr[slot][:, :].rearrange("p (c d) -> p c d", c=16)
            D3d = Vd[slot][:, :].rearrange("p (c d) -> p c d", c=16)
            if src[0] == "P":
                _, h_, rb = src
                DM("sp", KTr[slot][:, :], KTp[h_, :, rb * 2048:(rb + 1) * 2048],
                   [("KTp", g_) for g_ in range(rb * 4, rb * 4 + 4)], ["KTr%d" % slot], ("KTr", slot))
                DM("sp", Vr[slot][:, :], Vp[h_, rb],
                   [("Vp", rb, a, b_) for a in range(4) for b_ in range(4)], ["Vr%d" % slot], ("Vr", slot))
                vp_rd = [("Vp", rb, a, b_) for a in range(4) for b_ in range(4)]
                DM("sp", Vd[slot][0:127, :], Vp[h_, rb, 1:128, :], vp_rd, ["Vd%d" % slot, ("Vd%d" % slot, 1)], ("Vd", slot))
                DM("sp", Vd[slot][127:128, 0:960], Vp[h_, rb, 0:1, 64:1024], vp_rd, [("Vd%d" % slot, 2)], ("Vd", slot))
                if rb < 3:
                    DM("sp", Vd[slot][127:128, 960:1024], Vp[h_, rb + 1, 0:1, 0:64], [("Vp", rb + 1, 0, 0)], [("Vd%d" % slot, 3)], ("Vd", slot))
                else:
                    DM("sp", Vd[slot][127:128, 960:1024], zrow[:, 0:64], ["zrow"], [("Vd%d" % slot, 4)], ("Vd", slot))
            else:
                _, i_, h_ = src
                DM("sp", KTr[slot][:, :], KTs[i_, h_, :, 32:32 + PAST],
                   [("KTs", i_, b_) for b_ in range(4)], ["KTr%d" % slot], ("KTr", slot))
                for hf in range(2):
                    DM("sp", V3d[:, hf * 8:(hf + 1) * 8, :],
                       Vs[i_, hf * 1024:(hf + 1) * 1024, h_ * 64:(h_ + 1) * 64].rearrange("(c p) d -> p c d", p=128),
                       [("Vs", i_, hf)], ["Vr%d" % slot], ("Vr", slot))
                DM("sp", D3d[:, 0:8, :], Vs[i_, 1:1025, h_ * 64:(h_ + 1) * 64].rearrange("(c p) d -> p c d", p=128),
                   [("Vs", i_, 0), ("Vs", i_, 1)], ["Vd%d" % slot, ("Vd%d" % slot, 5)], ("Vd", slot))
                DM("sp", D3d[:, 8:15, :], Vs[i_, 1025:1921, h_ * 64:(h_ + 1) * 64].rearrange("(c p) d -> p c d", p=128),
                   [("Vs", i_, 1)], [("Vd%d" % slot, 6)], ("Vd", slot))
                DM("sp", D3d[0:127, 15, :], Vs[i_, 1921:2048, h_ * 64:(h_ + 1) * 64], [("Vs", i_, 1)], [("Vd%d" % slot, 7)], ("Vd", slot))
                DM("sp", D3d[127:128, 15, :], zrow[:, 0:64], ["zrow"], [("Vd%d" % slot, 8)], ("Vd", slot))
            I("pool", "tensor_tensor", ["Vr%d" % slot] + [("Vd%d" % slot, n_) for n_ in range(1, 10)], ["Vd%d" % slot] + [("Vd%d" % slot, n_) for n_ in range(1, 10)], out=Vd[slot][:, :], in0=Vd[slot][:, :], in1=Vr[slot][:, :], op=ALU.subtract)

        pendingC = None
        pf_queue = []
        for st in range(len(steps) + 2):
            newp = None
            want_prefetch = None
            if st < len(steps):
                ci, bi = steps[st]
                ch = chains[ci]
                blk = ch["blocks"][bi]
                R, nq, nk, h = ch["R"], ch["nq"], blk["nk"], ch["h"]
                rbase = (ci % 4) if kind == "S" else 0
                src = blk["src"]
                if src[0] == "N":
                    _, i_, h_ = src
                    DM("sp", KTn[:, :], KTs[i_, h_, :, 0:32], [("KTs", i_, "n")], ["KTn"], "KTn")
                    DM("sp", Vn[:, :], Vsn[i_, h_], [("Vsn", i_)], ["Vn"], "Vn")
                    kt_ap = KTn[0:64, 0:32]
                    ktk, vk = "KTn", "Vn"
                    vch = [(Vn[0:32, :], 32)]
                    abel = False
                    vrow0 = None
                else:
                    if src not in kv_slot:
                        issue_kv(src)
                    k_ = src_order.index(src)
                    if k_ + 1 < len(src_order) and src_order[k_ + 1] not in kv_slot:
                        want_prefetch = src_order[k_ + 1]
                    cur_src = [src, kv_slot[src]]
                    slot = cur_src[1]
                    sub = blk["sub"]
                    kt_ap = KTr[slot][0:64, sub * 512:(sub + 1) * 512]
                    ktk, vk = "KTr%d" % slot, "Vr%d" % slot
                    V3 = Vr[slot][:, :].rearrange("p (c d) -> p c d", c=16)
                    D3 = Vd[slot][:, :].rearrange("p (c d) -> p c d", c=16)
                    abel = bi >= 2
                    if abel:
                        vch = [(D3[:, sub * 4 + c, :], 128) for c in range(4)]
                        vk = "Vd%d" % slot
                    else:
                        vch = [(V3[:, sub * 4 + c, :], 128) for c in range(4)]
                    vrow0 = (V3[0:1, sub * 4, :], "Vr%d" % slot)
                s_ = bi % 2
                ic = 512 * s_ + 512 - nk
                nblk = len(ch["blocks"])
                abufs = []
                casts = []
                for r in range(R):
                    rr = (rbase + r) % 4
                    cpt, cpk = cp[rr], "cp%d" % rr
                    if bi == 0:
                        I("dve", "memset", [], [(cpk, "c", s_), (cpk, "s", s_)], ap=cpt[:, ic:ic + 1], constant=1.0)
                    bank, bk = PS[rr], "ps%d" % rr
                    qa = qT[0:64, h, ch["q0"] + r * nq: ch["q0"] + (r + 1) * nq]
                    I("pe", "matmul", ["qT", ktk], [bk], out=bank[0:nq, 0:nk], lhsT=qa, rhs=kt_ap, start=True, stop=True)
                    oi = rr_["om"] % 4
                    rr_["om"] += 1
                    omt, omk = om[oi], "om%d" % oi
                    I("act", "activation", [bk], [omk], out=omt[0:nq, 0:nk], in_=bank[0:nq, 0:nk], func=AF.Sigmoid, scale=-0.125)
                    rdc = [(cpk, "c", s_)] if (bi == 0 or s_ == 0) else [(cpk, "s", 1 - s_)]
                    I("dve", "tensor_tensor_scan", [omk, "mbt", "zmask"] + rdc, [(cpk, "s", s_)],
                      out=cpt[0:nq, ic + 1: ic + 1 + nk], data0=omt[0:nq, 0:nk], data1=blk["mask"][r], initial=cpt[0:nq, ic:ic + 1],
                      op0=ALU.mult, op1=ALU.max)
                    ai = s_ * 4 + rr
                    at_, ak = Ab[ai], "Ab%d" % ai
                    if abel:
                        casts.append((cpk, s_, at_, ak, cpt[0:nq, ic + 1: ic + 1 + nk], at_[0:nq, 0:nk]))
                    else:
                        I("dve", "tensor_tensor", [(cpk, "s", s_)] + rdc, [ak], out=at_[0:nq, 0:nk], in0=cpt[0:nq, ic: ic + nk],
                          in1=cpt[0:nq, ic + 1: ic + 1 + nk], op=ALU.subtract)
                        if bi == 1 and nblk > 2:
                            I("dve", "tensor_copy", [(cpk, "s", s_), ak], [ak], out=at_[0:nq, 512:513], in_=cpt[0:nq, ic + nk: ic + nk + 1])
                    if s_ == 1 and bi + 1 < nblk:
                        I("pool", "tensor_copy", [(cpk, "s", 1)], [(cpk, "c", 0)], out=cpt[0:nq, 0:1], in_=cpt[0:nq, 1024:1025])
                    abufs.append((at_, ak))
                newp = dict(q0=ch["q0"], bi=bi, nblk=nblk, abufs=abufs, vch=vch, vk=vk, R=R, nq=nq, nk=nk, h=h, ci=ci, abel=abel,
                            vrow0=vrow0, casts=casts)
                if pending is not None and pending["bi"] == 1 and pending["nblk"] > 2 and pending["ci"] == ci:
                    pending["next_vrow0"] = vrow0
            if pendingC is not None:
                for (rd_, wk_, kw_) in pendingC["mml"]:
                    I("pe", "matmul", rd_, [wk_], **kw_)
                if pendingC["fin"] is not None:
                    o_, i_ap, k_ = pendingC["fin"]
                    copy_op("dve", o_, i_ap, [k_], ["attnT"])
            newC = None
            if pending is not None:
                pd = pending
                R2, nq2, h2 = pd["R"], pd["nq"], pd["h"]
                acc, acck = (PS[4], "ps4") if pd["ci"] % 2 == 0 else (PS[5], "ps5")
                ncs = len(pd["vch"])
                mml = []
                for (cpk_, s__, at__, ak_, src_ap, dst_ap) in pd["casts"]:
                    I("act", "copy", [(cpk_, "s", s__)], [ak_], out=dst_ap, in_=src_ap)
                for cp_ in range((ncs + 1) // 2):
                    pt, pk = ptb()
                    cl = [c for c in (2 * cp_, 2 * cp_ + 1) if c < ncs]
                    for ci_, c in enumerate(cl):
                        va, kp = pd["vch"][c]
                        for r in range(R2):
                            at_, ak = pd["abufs"][r]
                            I("pe", "transpose", [ak, "ident"], [pk], out=pt[0:kp, ci_ * 512 + r * nq2: ci_ * 512 + (r + 1) * nq2],
                              in_=at_[0:nq2, c * 128: c * 128 + kp], identity=ident[0:nq2, 0:nq2])
                    ati = rr_["at"] % 4
                    rr_["at"] += 1
                    att, atk = ATb[ati], "ATb%d" % ati
                    kp0 = pd["vch"][cl[0]][1]
                    ev_e = "dve" if (pd["abel"] and cp_ == 1 and pd["bi"] % 2 == 0) else "act"
                    if len(cl) == 2:
                        copy_op(ev_e, att[0:kp0, :].rearrange("p (k t) -> p k t", k=2)[:, :, 0:R2 * nq2],
                                pt[0:kp0, :].rearrange("p (k t) -> p k t", k=2)[:, :, 0:R2 * nq2], [pk], [atk])
                    else:
                        copy_op("act", att[0:kp0, 0:R2 * nq2], pt[0:kp0, 0:R2 * nq2], [pk], [atk])
                    for ci_, c in enumerate(cl):
                        va, kp = pd["vch"][c]
                        first = (pd["bi"] == 0 and c == 0)
                        last = (pd["bi"] == pd["nblk"] - 1 and c == ncs - 1)
                        mml.append(([pd["vk"], atk], acck, dict(out=acc[0:64, 0:R2 * nq2], lhsT=va, rhs=att[0:kp, ci_ * 512: ci_ * 512 + R2 * nq2], start=first, stop=last)))
                if pd.get("next_vrow0") is not None:
                    vr_ap, vr_k = pd["next_vrow0"]
                    pt, pk = ptb()
                    for r in range(R2):
                        at_, ak = pd["abufs"][r]
                        I("pe", "transpose", [ak, "ident"], [pk], out=pt[0:1, r * nq2:(r + 1) * nq2], in_=at_[0:nq2, 512:513], identity=ident[0:nq2, 0:nq2])
                    ati = rr_["at"] % 4
                    rr_["at"] += 1
                    att, atk = ATb[ati], "ATb%d" % ati
                    copy_op("act", att[0:1, 0:R2 * nq2], pt[0:1, 0:R2 * nq2], [pk], [atk])
                    mml.append(([vr_k, atk], acck, dict(out=acc[0:64, 0:R2 * nq2], lhsT=vr_ap, rhs=att[0:1, 0:R2 * nq2], start=False, stop=False)))
                fin = None
                if pd["bi"] == pd["nblk"] - 1:
                    q0 = pd["q0"]
                    fin = (attnT[0:64, h2, q0: q0 + R2 * nq2], acc[0:64, 0:R2 * nq2], acck)
                newC = dict(mml=mml, fin=fin)
            pendingC = newC
            pending = newp
            if want_prefetch is not None:
                pf_queue.append((st + 2, want_prefetch))
            while pf_queue and pf_queue[0][0] <= st:
                _, src_ = pf_queue.pop(0)
                if src_ not in kv_slot:
                    issue_kv(src_)

    def group(kind, j):
        NT_ = 512 if kind == "P" else 128
        NS_ = NT_ // 128
        L = 512 if kind == "P" else 32

        def uev(t, a, b):
            if kind == "P":
                return t[:, a:b]
            return t[:, 0:136].rearrange("p (s l) -> p s l", s=4)[:, :, a:b]

        def v3(ap2):
            if kind == "P":
                return ap2
            return ap2.rearrange("p (s l) -> p s l", s=4)
        for s in range(NS_):
            src = xown[j, s * 128:(s + 1) * 128, :] if kind == "P" else xsam[:, :]
            DM("sp", xr[:, s, :], src, [], [("xr", s)], ("xr", s))
        norm_to_hT(NS_, NT_)
        Wq, wqk = ws.next()
        Wq3 = Wq[:, :].rearrange("p (k c) -> p k c", k=8)
        for h in range(8):
            bank, bk = psb()
            mm_acc(bank[0:64, 0:NT_], bk, [(Wq3[:, kc, h * 64:(h + 1) * 64], hT[:, kc, 0:NT_], [("hT", kc)]) for kc in range(8)], [wqk])
            copy_op(evac_eng(), qT[0:64, h, 0:NT_], bank[0:64, 0:NT_], [bk], ["qT"])
        hTh3 = hTh[:, :].rearrange("p (k t) -> p k t", k=8)
        if kind == "P":
            DM("sp", xh[:, :], xhalo[j], [], ["xh"], "xh")
            I("act", "activation", ["xh"], ["junk", "ssh"], out=junk[0:2, :], in_=xh[:, :], func=AF.Square, accum_out=ssh[0:2, :])
            I("act", "activation", ["ssh", "epsb"], ["sqh"], out=sqh[0:2, :], in_=ssh[0:2, :], func=AF.Sqrt, bias=epsb[0:2, 0:1], scale=1.0 / D)
            I("dve", "reciprocal", ["sqh"], ["rsh"], out=rsh[0:2, :], in_=sqh[0:2, :])
            I("dve", "tensor_scalar", ["xh", "rsh"], ["hhb"], out=hhb[:, :], in0=xh[:, :], scalar1=rsh[0:2, 0:1], scalar2=None, op0=ALU.mult)
            pt, pk = ptb()
            for kc in range(8):
                I("pe", "transpose", ["hhb", "ident"], [pk], out=pt[:, kc * 2: kc * 2 + 2], in_=hhb[0:2, kc * 128:(kc + 1) * 128], identity=ident[0:2, 0:2])
            copy_op("dve", hTh[:, :], pt[:, 0:16], [pk], ["hTh"])
        for c in range(4):
            Wc, wck = ws.next()
            Wc3 = Wc[:, 0:3072].rearrange("p (k c) -> p k c", k=8)
            banks = [psb() for _ in range(3)]
            for t_ in range(3):
                mm_acc(banks[t_][0][:, 0:NT_], banks[t_][1], [(Wc3[:, kc, t_ * 128:(t_ + 1) * 128], hT[:, kc, 0:NT_], [("hT", kc)]) for kc in range(8)], [wck])
            i2 = c % 2
            uet, uek = ue[i2], "ue%d" % i2
            if kind == "P":
                bh, bhk = psb()
                for t_ in (1, 2):
                    mm_acc(bh[:, (t_ - 1) * 2:(t_ - 1) * 2 + 2], bhk, [(Wc3[:, kc, t_ * 128:(t_ + 1) * 128], hTh3[:, kc, :], ["hTh"]) for kc in range(8)], [wck])
                I("act", "copy", [bhk], ["cxh"], out=cxh[:, :], in_=bh[:, 2:4])
                I("dve", "tensor_tensor", [bhk, "cxh"], [(uek, "h")], out=uet[:, 512:514], in0=bh[:, 0:2], in1=cxh[:, :], op=ALU.mult)
            else:
                DM("sp", uev(uet, 32, 34), cconvT[c * 128:(c + 1) * 128, :].rearrange("p (s t) -> p s t", s=4), [], [(uek, "h")], ("ueh", i2))
            (bcb, bcbk), (bcc, bcck), (bcx, bcxk) = banks
            I("act", "copy", [bcxk], ["cxs%d" % i2], out=cxs[i2][:, 0:NT_], in_=bcx[:, 0:NT_])
            I("dve", "tensor_tensor", [bcck, "cxs%d" % i2], [(uek, "m")], out=uev(uet, 0, L), in0=v3(bcc[:, 0:NT_]), in1=v3(cxs[i2][:, 0:NT_]), op=ALU.mult)
            if kind == "S":
                DM("sp", conv_s[:, c, :, :], uev(uet, 0, 2), [(uek, "m")], [], ("ueo", i2))
            elif j == 0:
                DM("sp", conv_p[:, c, :], uet[:, 0:2], [(uek, "m")], [], ("ueo", i2))
            yt, yk = yac[i2], "yac%d" % i2
            I("dve", "tensor_scalar", [(uek, "m"), "cw"], [yk], out=v3(yt[:, 0:NT_]), in0=uev(uet, 0, L), scalar1=cw[:, c, 2:3], scalar2=None, op0=ALU.mult)
            for (sh, wi) in ((1, 1), (2, 0)):
                I("dve", "scalar_tensor_tensor", [(uek, "m"), (uek, "h"), "cw", yk], [yk], out=v3(yt[:, 0:NT_]), in0=uev(uet, sh, L + sh),
                  scalar=cw[:, c, wi:wi + 1], in1=v3(yt[:, 0:NT_]), op0=ALU.mult, op1=ALU.add)
            I("dve", "tensor_tensor", [bcbk, yk], [("ycb", c)], out=ycb[:, c, 0:NT_], in0=bcb[:, 0:NT_], in1=yt[:, 0:NT_], op=ALU.mult)
        attention(kind, j)
        for c in range(8):
            Wm, wmk = ws.next()
            Wg3 = Wm[:, 0:2048].rearrange("p (k c) -> p k c", k=8)
            Wa3 = Wm[0:64, 2048:3072].rearrange("p (h m) -> p h m", h=8)
            Wo3 = Wm[:, 3072:3584].rearrange("p (k m) -> p k m", k=4)
            (bga, bgak), (bgc, bgck), (bya, byak), (byc, byck) = [psb() for _ in range(4)]
            mm_acc(bga[:, 0:NT_], bgak, [(Wg3[:, kc, 0:128], hT[:, kc, 0:NT_], [("hT", kc)]) for kc in range(8)], [wmk])
            mm_acc(bgc[:, 0:NT_], bgck, [(Wg3[:, kc, 128:256], hT[:, kc, 0:NT_], [("hT", kc)]) for kc in range(8)], [wmk])
            mm_acc(bya[:, 0:NT_], byak, [(Wa3[:, h, :], attnT[0:64, h, 0:NT_], ["attnT"]) for h in range(8)], [wmk])
            mm_acc(byc[:, 0:NT_], byck, [(Wo3[:, kc, :], ycb[:, kc, 0:NT_], [("ycb", kc)]) for kc in range(4)], [wmk])
            i2 = c % 2
            I("act", "activation", [bgak], ["cxs%d" % i2], out=sga[i2][:, 0:NT_], in_=bga[:, 0:NT_], func=AF.Sigmoid)
            I("act", "activation", [bgck], ["yac%d" % i2], out=sgc[i2][:, 0:NT_], in_=bgc[:, 0:NT_], func=AF.Sigmoid)
            I("dve", "tensor_tensor", [byak, "cxs%d" % i2], ["t1_%d" % i2], out=t1[i2][:, 0:NT_], in0=bya[:, 0:NT_], in1=sga[i2][:, 0:NT_], op=ALU.mult)
            I("dve", "tensor_tensor", [byck, "yac%d" % i2], ["yac%d" % i2], out=sgc[i2][:, 0:NT_], in0=byc[:, 0:NT_], in1=sgc[i2][:, 0:NT_], op=ALU.mult)
            I("dve", "tensor_tensor", ["t1_%d" % i2, "yac%d" % i2], [("mT", c)], out=mT[:, c, 0:NT_], in0=t1[i2][:, 0:NT_], in1=sgc[i2][:, 0:NT_], op=ALU.add)
        for jc in range(2):
            Wo_, wok = ws.next()
            W3 = Wo_[:, :].rearrange("p (k c) -> p k c", k=8)
            for s in range(NS_):
                bank, bk = psb()
                mm_acc(bank[:, :], bk, [(mT[:, kc, s * 128:(s + 1) * 128], W3[:, kc, :], [("mT", kc)]) for kc in range(8)], [wok])
                xs_ = xr[:, s, jc * 512:(jc + 1) * 512]
                I("dve", "tensor_tensor", [bk, ("xr", s)], [("xr", s)], out=xs_, in0=xs_, in1=bank[:, :], op=ALU.add)
        norm_to_hT(NS_, NT_)
        for pu in range(8):
            Wu, wuk = ws.next()
            W3 = Wu[:, :].rearrange("p (k c) -> p k c", k=8)
            for m in range(4):
                bank, bk = psb()
                mm_acc(bank[:, 0:NT_], bk, [(W3[:, kc, m * 128:(m + 1) * 128], hT[:, kc, 0:NT_], [("hT", kc)]) for kc in range(8)], [wuk])
                i2 = m % 2
                I("act", "activation", [bk], ["t1_%d" % i2], out=t1[i2][:, 0:NT_], in_=bank[:, 0:NT_], func=AF.Relu)
                wr = [("fT", pu * 4 + m)] + (REGB + ["regB_tok"] if (pu == 0 and m == 0) else [])
                I("pool", "tensor_tensor", ["t1_%d" % i2], wr, out=fT[:, pu * 4 + m, 0:NT_], in0=t1[i2][:, 0:NT_], in1=t1[i2][:, 0:NT_], op=ALU.mult)
        for chh in range(2):
            bks = [psb() for _ in range(NS_)]
            for kq in range(4):
                Wd, wdk = ws.next()
                W3 = Wd[:, :].rearrange("p (k c) -> p k c", k=8)
                for s in range(NS_):
                    bank, bk = bks[s]
                    for kc in range(8):
                        I("pe", "matmul", [("fT", kq * 8 + kc), wdk], [bk], out=bank[:, :], lhsT=fT[:, kq * 8 + kc, s * 128:(s + 1) * 128], rhs=W3[:, kc, :],
                          start=(kq == 0 and kc == 0), stop=(kq == 3 and kc == 7))
            for s in range(NS_):
                bank, bk = bks[s]
                xs_ = xr[:, s, chh * 512:(chh + 1) * 512]
                I("dve", "tensor_tensor", [bk, ("xr", s)], [("xr", s)], out=xs_, in0=xs_, in1=bank[:, :], op=ALU.add)
        norm_to_hT(NS_, NT_)
        psrc = pown[j].rearrange("(s p) f -> p s f", p=128) if kind == "P" else psam.rearrange("(s p) f -> p s f", p=128)
        DM("sp", pst[:, 0:NS_, :], psrc, ["regB_tok"], ["pst"], "pst")
        I("dve", "tensor_copy", ["pst"], ["pbf"], out=pbf[:, 0:NS_, :], in_=pst[:, 0:NS_, :])
        pt, pk = ptb()
        for k2 in range(2):
            for s in range(NS_):
                I("pe", "transpose", ["pbf", "ident"], [pk], out=pt[:, k2 * 512 + s * 128: k2 * 512 + (s + 1) * 128], in_=pbf[:, s, k2 * 128:(k2 + 1) * 128], identity=ident[:])
        copy_op(evac_eng(), pT[:, 0:2, 0:NT_], pt[:, :].rearrange("p (k t) -> p k t", k=2)[:, :, 0:NT_], [pk], [("pT", 0), ("pT", 1)])
        for jc in range(2):
            Wg, wgk = ws.next()
            W3 = Wg[:, :].rearrange("p (k c) -> p k c", k=8)
            for s in range(NS_):
                bank, bk = psb()
                mm_acc(bank[:, :], bk, [(hT[:, kc, s * 128:(s + 1) * 128], W3[:, kc, :], [("hT", kc)]) for kc in range(8)], [wgk])
                I("act", "activation", [bk], [("gate", s)], out=gate[:, s, :], in_=bank[:, :], func=AF.Sigmoid)
            Wl, wlk = ws.next()
            Wl3 = Wl[:, 0:1024].rearrange("p (k c) -> p k c", k=2)
            for s in range(NS_):
                bank, bk = psb()
                mm_acc(bank[:, :], bk, [(pT[:, k2, s * 128:(s + 1) * 128], Wl3[:, k2, :], [("pT", k2)]) for k2 in range(2)], [wlk])
                I("dve", "tensor_tensor", [bk, ("gate", s)], [("gate", s)], out=gate[:, s, :], in0=bank[:, :], in1=gate[:, s, :], op=ALU.mult)
                xs_ = xr[:, s, jc * 512:(jc + 1) * 512]
                I("dve", "tensor_tensor", [("gate", s), ("xr", s)], [("xr", s)], out=xs_, in0=xs_, in1=gate[:, s, :], op=ALU.add)
        norm_stats(NS_)
        for s in range(NS_):
            i = s % 2
            I("dve", "scalar_tensor_tensor", [("xr", s), "rstd", "gfin"], ["stg%d" % i, ("stgq", 2 * i), ("stgq", 2 * i + 1)], out=stg[i][:, :], in0=xr[:, s, :], scalar=rstd[:, s:s + 1],
              in1=gfin[:, :], op0=ALU.mult, op1=ALU.mult)
            dst = y_own[j, s * 128:(s + 1) * 128, :] if kind == "P" else y_s[:, :]
            DM("sp", dst, stg[i][:, :], ["stg%d" % i], [], ("stg", i))
        I("pool", "memset", [], [("fT", i) for i in range(32)] + ["pst", "pbf", ("pT", 0), ("pT", 1)] + [("gate", s) for s in range(4)] + REGB,
          ap=cp[0][:, 0:1], constant=1.0)

    grp_pieces = [(PQ, 4096)] + [(PCONV + c, 3072) for c in range(4)] + [(PMIX + c, 3584) for c in range(8)] + \
                 [(PO, 4096), (PO + 1, 4096)] + [(PUP + i, 4096) for i in range(8)] + [(PDN + i, 4096) for i in range(8)] + \
                 [(PPG, 4096), (PPLE, 1024), (PPG + 1, 4096), (PPLE + 1, 1024)]
    for _ in range(17):
        ws.plan([(PK, 4096), (PV, 4096)])
    for _ in range(9):
        ws.plan(grp_pieces)

    prep_unscaled()
    build_prep_items()
    pi = [0]

    def run_prep(n):
        for _ in range(n):
            if pi[0] < len(prep_items):
                prep_items[pi[0]]()
                pi[0] += 1
    STAGE = _STAGE[0]
    run_prep(16)
    if STAGE >= 2:
        for g in range(NG if _SUB[0] == 0 else _NGRP[0]):
            phase1a("P", g)
            run_prep(6)
        if _SUB[0] in (0, 8):
            phase1a("S", 0)
    run_prep(len(prep_items))
    if STAGE >= 3:
        cache_prep()
    I("pool", "memset", [], ["wst0", "wst1", "wbf0", "wbf1", "ckb0", "ckb1"] + [("xr2", s_) for s_ in range(4)] + REGB, ap=cp[0][:, 0:1], constant=1.0)
    if STAGE >= 4:
        for j in range(8 if STAGE >= 5 else 1):
            group("P", j)
    if STAGE >= 6:
        group("S", 0)
    P.emit()
    print("instructions recorded:", len(P.ins))
    return nc


def _fix_none_keys():
    pass


_NC = [None]


def kernel(x_prompt, x_sample, p_prompt, p_sample, cache_k, cache_v, cache_conv,
           g_mix, w_in, conv_w, w_attn_out, w_conv_out, w_o,
           g_ffn, w_up, w_down, g_ple, w_ple_gate, w_ple, g_final):
    f = lambda a: np.ascontiguousarray(np.asarray(a, dtype=np.float32))
    x_prompt, x_sample, p_prompt, p_sample = f(x_prompt), f(x_sample), f(p_prompt), f(p_sample)
    cache_k, cache_v, cache_conv = f(cache_k), f(cache_v), f(cache_conv)
    if _NC[0] is None:
        _NC[0] = build_program()
    nc = _NC[0]
    diag = np.zeros((4, 128, 512), np.float32)
    for r in range(4):
        diag[r] = (np.arange(512)[None, :] <= (128 * r + np.arange(128))[:, None]).astype(np.float32)
    ones = np.ones((128, 512), np.float32)
    zeros = np.zeros((128, 512), np.float32)
    gvec = np.stack([np.asarray(g, np.float32).reshape(8, 128).T for g in (g_mix[0], g_ffn[0], g_ple[0])], axis=1).reshape(128, 24)
    shared = dict(
        gfin=f(np.broadcast_to(np.asarray(g_final, np.float32), (128, D))),
        convwT=f(np.asarray(conv_w[0], np.float32).T),
        gvec=f(gvec),
        w_in=f(w_in[0]), w_ao=f(w_attn_out[0]), w_co=f(w_conv_out[0]), w_o=f(w_o[0]),
        w_up=f(w_up[0]), w_dn=f(w_down[0]), w_pg=f(w_ple_gate[0]), w_ple=f(w_ple[0]),
    )
    in_maps = []
    for c in range(8):
        b, p = c // 2, c % 2
        xs = x_prompt[b, ::-1]
        ps = p_prompt[0, b, ::-1]
        xown = np.stack([xs[512 * g:512 * g + 512] for g in GRP[p]])
        pown = np.stack([ps[512 * g:512 * g + 512] for g in GRP[p]])
        xhalo = np.zeros((8, 2, D), np.float32)
        for j, g in enumerate(GRP[p]):
            if g < 15:
                xhalo[j] = xs[512 * (g + 1): 512 * (g + 1) + 2]
        mbank = np.zeros((17, 128, 512), np.float32)
        for jpar in range(2):
            caseA = (p == 0 and jpar == 0) or (p == 1 and jpar == 1)
            for r in range(4):
                mbank[(jpar * 2 + 0) * 4 + r] = diag[r] if caseA else ones
                mbank[(jpar * 2 + 1) * 4 + r] = zeros if caseA else diag[r]
        mbank[16] = diag[0]
        sl = slice(4 * c, 4 * c + 4)
        xsam = x_sample[sl, ::-1].reshape(128, D)
        psam = p_sample[0, sl, ::-1].reshape(128, 256)
        ckc = cache_k[0, sl, ::-1].reshape(4, PAST, 512)
        cvc = cache_v[0, sl, ::-1].reshape(4, PAST, 512)
        cc = cache_conv[0, sl, ::-1]
        cconvT = np.transpose(cc, (2, 0, 1)).reshape(512, 8)
        m = dict(xall=f(xs), xown=f(xown), xhalo=f(xhalo), pown=f(pown), xsam=f(xsam), psam=f(psam),
                 ck=f(ckc), cv=f(cvc), cconvT=f(cconvT), mb=f(mbank))
        m.update(shared)
        in_maps.append(m)
    ncr = _NCORES[0]
    res = run_bass_kernel_spmd(nc, in_maps[:ncr], core_ids=list(range(ncr)))
    R = list(res.results) + [res.results[0]] * (8 - ncr)
    y_prompt = np.zeros((4, T, D), np.float32)
    k_prompt = np.zeros((1, 4, T, 8, 64), np.float32)
    v_prompt = np.zeros((1, 4, T, 8, 64), np.float32)
    conv_prompt = np.zeros((1, 4, 2, 512), np.float32)
    y_sample = np.zeros((32, 32, D), np.float32)
    k_sample = np.zeros((1, 32, 32, 8, 64), np.float32)
    v_sample = np.zeros((1, 32, 32, 8, 64), np.float32)
    conv_sample = np.zeros((1, 32, 2, 512), np.float32)
    for c in range(8):
        b, p = c // 2, c % 2
        r = R[c]
        ys = np.zeros((T, D), np.float32) if p == 0 else None
        for j, g in enumerate(GRP[p]):
            y_prompt[b, T - 512 * (g + 1): T - 512 * g] = np.asarray(r["y_own"][j])[::-1]
        if p == 0:
            k_prompt[0, b] = np.asarray(r["k_all"])[::-1].reshape(T, 8, 64)
            v_prompt[0, b] = np.asarray(r["v_all"])[::-1].reshape(T, 8, 64)
            cp_ = np.asarray(r["conv_p"])
            u = np.transpose(cp_, (1, 0, 2)).reshape(512, 2)
            conv_prompt[0, b, 0] = u[:, 1]
            conv_prompt[0, b, 1] = u[:, 0]
        sl = slice(4 * c, 4 * c + 4)
        y_sample[sl] = np.asarray(r["y_s"]).reshape(4, 32, D)[:, ::-1]
        k_sample[0, sl] = np.asarray(r["k_s"]).reshape(4, 32, 8, 64)[:, ::-1]
        v_sample[0, sl] = np.asarray(r["v_s"]).reshape(4, 32, 8, 64)[:, ::-1]
        cs = np.asarray(r["conv_s"])
        u = np.transpose(cs, (2, 1, 0, 3)).reshape(4, 512, 2)
        conv_sample[0, sl, 0] = u[:, :, 1]
        conv_sample[0, sl, 1] = u[:, :, 0]
    return (y_prompt, y_sample, k_prompt, v_prompt, conv_prompt, k_sample, v_sample, conv_sample)
```

```python
import numpy as np
import concourse.bass as bass
import concourse.mybir as mybir
from concourse.bass_utils import run_bass_kernel_spmd

F32 = mybir.dt.float32
BF16 = mybir.dt.bfloat16
AF = mybir.ActivationFunctionType
ALU = mybir.AluOpType

D = 1024
NPROJ = 5120
T = 8192
NG = 16
GRP = {0: [0, 3, 4, 7, 8, 11, 12, 15], 1: [1, 2, 5, 6, 9, 10, 13, 14]}
EPS = 1e-6
PAST = 2048

PK, PV, PQ, PCONV, PMIX, PO, PUP, PDN, PPG, PPLE, NPIECE = 0, 1, 2, 3, 7, 15, 17, 25, 33, 35, 37


_STAGE = [9]
_SUB = [0]
_NGRP = [1]
_NCORES = [8]


class Prog:
    def __init__(self, nc):
        self.nc = nc
        self.ins = []
        self.lastw = {}
        self.reads = {}

    def _add(self, eng, fn, reads, writes, dma, semkey):
        idx = len(self.ins)
        deps = set()
        for b in reads:
            if b in self.lastw:
                deps.add(self.lastw[b])
        for b in writes:
            if b in self.lastw:
                deps.add(self.lastw[b])
            for r in self.reads.get(b, ()):
                deps.add(r)
        self.ins.append(dict(eng=eng, fn=fn, deps=deps, dma=dma, semkey=semkey))
        for b in reads:
            self.reads.setdefault(b, []).append(idx)
        for b in writes:
            self.lastw[b] = idx
            self.reads[b] = []
        return idx

    def op(self, eng, fn, reads=(), writes=()):
        return self._add(eng, fn, reads, writes, False, None)

    def dma(self, eng, fn, reads=(), writes=(), semkey=None):
        assert semkey is not None
        return self._add(eng, fn, reads, writes, True, semkey)

    def emit(self):
        nc = self.nc
        engs = dict(pe=nc.tensor, act=nc.scalar, dve=nc.vector, pool=nc.gpsimd, sp=nc.sync)
        ins = self.ins
        needed = set()
        for i, it in enumerate(ins):
            for d in it["deps"]:
                pd = ins[d]
                if pd["dma"] or pd["eng"] != it["eng"] or it["dma"] or pd["eng"] != "pe":
                    needed.add(d)
        sems = {}

        def getsem(k):
            if k not in sems:
                sems[k] = nc.alloc_semaphore("s%d" % len(sems))
            return sems[k]
        cnt = {}
        comp = {}
        waited = {}
        last_dma = {}
        gtot = {}
        for it in ins:
            if it["dma"] and isinstance(it["semkey"], tuple) and it["semkey"][0] == "grp":
                gtot[it["semkey"]] = gtot.get(it["semkey"], 0) + 16
        for i, it in enumerate(ins):
            eng = it["eng"]
            e = engs[eng]
            best = {}
            for d in it["deps"]:
                if d not in comp:
                    continue
                sk, val = comp[d]
                if sk == ("e", "pe") and eng == "pe" and not it["dma"]:
                    continue
                if best.get(sk, 0) < val:
                    best[sk] = val
            for sk, val in best.items():
                if waited.get((eng, sk), 0) >= val:
                    continue
                e.wait_ge(getsem(sk), val)
                waited[(eng, sk)] = val
            r = it["fn"](e)
            if it["dma"]:
                sk = ("d", it["semkey"])
                cnt[sk] = cnt.get(sk, 0) + 16
                r.then_inc(getsem(sk), 16)
                comp[i] = (sk, gtot.get(it["semkey"], cnt[sk]))
                last_dma[sk] = cnt[sk]
            elif i in needed:
                sk = ("e", eng)
                cnt[sk] = cnt.get(sk, 0) + 1
                r.then_inc(getsem(sk), 1)
                comp[i] = (sk, cnt[sk])
        for sk, val in last_dma.items():
            if waited.get(("sp", sk), 0) < val:
                nc.sync.wait_ge(getsem(sk), val)


def build_program():
    nc = bass.Bass("TRN2", target_bir_lowering=False)
    P = Prog(nc)

    def I(eng, meth, reads, writes, **kw):
        P.op(eng, lambda e: getattr(e, meth)(**kw), reads=reads, writes=writes)

    def DM(q, out, in_, reads, writes, semkey):
        P.dma(q, lambda e: e.dma_start(out=out, in_=in_), reads=reads, writes=writes, semkey=semkey)

    def din(name, shape, dt=F32):
        return nc.dram_tensor(name, list(shape), dt, kind="ExternalInput").ap()

    def dout(name, shape, dt=F32):
        return nc.dram_tensor(name, list(shape), dt, kind="ExternalOutput").ap()

    def dscr(name, shape, dt=BF16):
        return nc.dram_tensor(name, list(shape), dt, kind="Internal").ap()

    xall = din("xall", [T, D])
    xown = din("xown", [8, 512, D])
    xhalo = din("xhalo", [8, 2, D])
    pown = din("pown", [8, 512, 256])
    xsam = din("xsam", [128, D])
    psam = din("psam", [128, 256])
    ck = din("ck", [4, PAST, 512])
    cv = din("cv", [4, PAST, 512])
    cconvT = din("cconvT", [512, 8])
    mb = din("mb", [17, 128, 512])
    gfin_d = din("gfin", [128, D])
    convwT = din("convwT", [512, 3])
    gvec_d = din("gvec", [128, 24])
    w_in = din("w_in", [D, NPROJ])
    w_ao = din("w_ao", [512, D])
    w_co = din("w_co", [512, D])
    w_o = din("w_o", [D, D])
    w_up = din("w_up", [D, 4096])
    w_dn = din("w_dn", [4096, D])
    w_pg = din("w_pg", [D, D])
    w_ple = din("w_ple", [256, D])

    y_own = dout("y_own", [8, 512, D])
    k_all = dout("k_all", [T, 512])
    v_all = dout("v_all", [T, 512])
    conv_p = dout("conv_p", [128, 4, 2])
    y_s = dout("y_s", [128, D])
    k_s = dout("k_s", [128, 512])
    v_s = dout("v_s", [128, 512])
    conv_s = dout("conv_s", [128, 4, 4, 2])

    wimg = dscr("wimg", [NPIECE, 128, 4096])
    KTp = dscr("KTp", [8, 64, T])
    Vp = dscr("Vp", [8, 4, 128, 1024])
    KTs = dscr("KTs", [4, 8, 64, 32 + PAST])
    Vs = dscr("Vs", [4, PAST, 512])
    Vsn = dscr("Vsn", [4, 8, 32, 64])
    Vq = dscr("Vq", [8, 4, 128, 1024])
    zrow = dscr("zrow", [1, 512])

    globals_hT = [None]
    globals_xr = [None]
    x_issued = set()
    SB_BASE = 16512
    off = [SB_BASE]

    def sb(name, shape, dt, at=None):
        nbytes = int(np.prod(shape[1:])) * (4 if dt == F32 else 2)
        nbytes = (nbytes + 31) // 32 * 32
        if at is None:
            o = off[0]
            off[0] += nbytes
        else:
            o = at
        assert o + nbytes <= SB_BASE + 212800, (name, o, nbytes)
        return nc.alloc_sbuf_tensor_at(name, list(shape), dt, offset=o)

    ident = sb("ident", [128, 128], BF16)
    zmask = sb("zmask", [128, 512], BF16)
    mbt = sb("mbt", [128, 17, 512], BF16)
    gfin = sb("gfin_s", [128, D], F32)
    cw = sb("cw", [128, 4, 3], F32)
    gvec = sb("gvec_s", [128, 24], F32)
    epsb = sb("epsb", [128, 1], F32)
    ss = sb("ss", [128, 4], F32)
    sq = sb("sq", [128, 4], F32)
    rstd = sb("rstd", [128, 4], F32)
    ssh = sb("ssh", [128, 1], F32)
    sqh = sb("sqh", [128, 1], F32)
    rsh = sb("rsh", [128, 1], F32)
    xr = sb("xr", [128, 4, D], F32)
    hbf = sb("hbf", [128, 4, D], BF16)
    hT = sb("hT", [128, 8, 512], BF16)
    qT = sb("qT", [64, 8, 512], BF16)
    attnT = sb("attnT", [64, 8, 512], BF16)
    ycb = sb("ycb", [128, 4, 512], BF16)
    mT = sb("mT", [128, 8, 512], BF16)
    xh = sb("xh", [2, D], F32)
    hhb = sb("hhb", [2, D], BF16)
    hTh = sb("hTh", [128, 16], BF16)
    cxh = sb("cxh", [128, 2], F32)
    globals_hT[0] = hT
    globals_xr[0] = xr
    wring = [sb("wring%d" % i, [128, 4096], BF16) for i in range(4)]
    cxs = [sb("cxs%d" % i, [128, 512], F32) for i in range(2)]
    ue = [sb("ue%d" % i, [128, 520], F32) for i in range(2)]
    yac = [sb("yac%d" % i, [128, 512], F32) for i in range(2)]
    sga = [cxs[0], cxs[1]]
    sgc = [yac[0], yac[1]]
    t1 = [sb("t1_%d" % i, [128, 512], F32) for i in range(2)]
    stg = [sb("stg%d" % i, [128, D], F32) for i in range(2)]
    junk = sb("junk", [128, D], BF16)
    regB = off[0]
    KTr = [sb("KTr%d" % i, [64, 2048], BF16) for i in range(2)]
    Vr = [sb("Vr%d" % i, [128, 1024], BF16) for i in range(2)]
    Vd = [sb("Vd%d" % i, [128, 1024], BF16) for i in range(2)]
    om = [sb("om%d" % i, [128, 512], F32) for i in range(4)]
    cp = [sb("cp%d" % i, [128, 1026], F32) for i in range(4)]
    Ab = [sb("Ab%d" % i, [128, 528], BF16) for i in range(8)]
    ATb = [sb("ATb%d" % i, [128, 1024], BF16) for i in range(4)]
    KTn = sb("KTn", [64, 32], BF16)
    Vn = sb("Vn", [32, 64], BF16)
    endB = off[0]
    fT = sb("fT", [128, 32, 512], BF16, at=regB)
    gate = sb("gate", [128, 4, 512], F32, at=regB + 32768)
    pst = sb("pst", [128, 4, 256], F32, at=regB + 32768 + 8192)
    pbf = sb("pbf", [128, 4, 256], BF16, at=regB + 32768 + 12288)
    pT = sb("pT", [128, 2, 512], BF16, at=regB + 32768 + 14336)
    assert regB + 32768 + 16384 <= endB
    assert regB + 32768 <= endB, (regB, endB)
    wst = [sb("wst%d" % i, [128, D], F32, at=regB + i * 4096) for i in range(2)]
    wbf = [sb("wbf%d" % i, [128, D], BF16, at=regB + 8192 + i * 2048) for i in range(2)]
    ckb = [sb("ckb%d" % i, [128, 4, 512], BF16, at=regB + 12288 + i * 4096) for i in range(2)]
    xr2 = sb("xr2", [128, 4, D], F32, at=regB + 20480)
    assert regB + 20480 + 16384 <= endB
    wst += [sb("wst%d" % i, [128, D], F32, at=regB + 36864 + (i - 2) * 4096) for i in range(2, 4)]
    wbf += [sb("wbf%d" % i, [128, D], BF16, at=regB + 45056 + (i - 2) * 2048) for i in range(2, 4)]
    assert regB + 49152 <= endB, (regB, endB)
    REGB = ["KTr0", "KTr1", "Vr0", "Vr1", "Vd0", "Vd1"] + ["om%d" % i for i in range(4)] + \
           ["Ab%d" % i for i in range(8)] + ["ATb%d" % i for i in range(4)] + ["KTn", "Vn"]
    for i in range(4):
        REGB += [("cp%d" % i, "c", 0), ("cp%d" % i, "s", 0), ("cp%d" % i, "s", 1)]
    print("SBUF used", off[0])

    PS = [nc.alloc_psum_tensor("ps%d" % i, [128, 512], F32) for i in range(6)]
    PT = [nc.alloc_psum_tensor("pt%d" % i, [128, 1024], BF16) for i in range(2)]
    rr_ = dict(ps=0, pt=0, ev=0, at=0, kv=0, pr=0, om=0, sg=0)

    def psb():
        i = rr_["ps"] % 6
        rr_["ps"] += 1
        return PS[i], "ps%d" % i

    def ptb():
        i = rr_["pt"] % 2
        rr_["pt"] += 1
        return PT[i], ("pt", i)

    def evac_eng():
        rr_["ev"] += 1
        return "act" if rr_["ev"] % 2 else "dve"

    def copy_op(eng, out, in_, reads, writes):
        if eng == "act":
            I("act", "copy", reads, writes, out=out, in_=in_)
        else:
            I(eng, "tensor_copy", reads, writes, out=out, in_=in_)

    I("pool", "memset", [], ["ident"], ap=ident[:], constant=0.0)
    I("pool", "affine_select", ["ident"], ["ident"], out=ident[:], in_=ident[:], pattern=[[-1, 128]],
      compare_op=ALU.not_equal, fill=1.0, base=0, channel_multiplier=1)
    I("pool", "memset", [], ["zmask"], ap=zmask[:], constant=0.0)
    I("pool", "memset", [], ["epsb"], ap=epsb[:], constant=EPS)
    DM("pool", mbt[:], mb.rearrange("m p k -> p m k"), [], ["mbt"], "mbt")
    DM("sp", gfin[:], gfin_d, [], ["gfin"], "gfin")
    DM("sp", cw[:], convwT.rearrange("(c p) j -> p c j", p=128), [], ["cw"], "cw")
    DM("sp", gvec[:], gvec_d, [], ["gvec"], "gvec")
    vq_keys = {rb: [] for rb in range(4)}
    DM("sp", zrow, zmask[0:1, 0:512], ["zmask"], ["zrow"], "zrow")

    parts = {}

    def wpart(piece, part):
        parts.setdefault(piece, set()).add(part)
        return ("wimg", piece, part)

    def img(piece, kc, ncol, c0, w):
        return wimg[piece, :, kc * ncol + c0: kc * ncol + c0 + w]

    def prep_unscaled():
        for j in range(2):
            src = w_o.rearrange("(k p) (j c) -> j p k c", p=128, j=2)[j]
            dst = wimg[PO + j].rearrange("p (k c) -> p k c", k=8)
            DM("pool", dst, src, [], [wpart(PO + j, 0)], ("grp", "pu"))
        for ch in range(2):
            for kq in range(4):
                pc = PDN + ch * 4 + kq
                src = w_dn[kq * 1024:(kq + 1) * 1024, ch * 512:(ch + 1) * 512].rearrange("(k p) c -> p k c", p=128)
                dst = wimg[pc].rearrange("p (k c) -> p k c", k=8)
                DM("pool", dst, src, [], [wpart(pc, 0)], ("grp", "pu"))
        for j in range(2):
            src = w_ple[:, j * 512:(j + 1) * 512].rearrange("(k p) c -> p k c", p=128)
            dst = wimg[PPLE + j, :, 0:1024].rearrange("p (k c) -> p k c", k=2)
            DM("pool", dst, src, [], [wpart(PPLE + j, 0)], ("grp", "pu"))
        for c in range(8):
            src = w_ao[:, c * 128:(c + 1) * 128].rearrange("(h q) m -> q h m", q=64)
            dst = wimg[PMIX + c, 0:64, 2048:3072].rearrange("q (h m) -> q h m", h=8)
            DM("pool", dst, src, [], [wpart(PMIX + c, "ao")], ("grp", "pu"))
            dst2 = wimg[PMIX + c, 64:128, 2048:3072].rearrange("q (h m) -> q h m", h=8)
            DM("pool", dst2, src, [], [wpart(PMIX + c, "ao2")], ("grp", "pu"))
            src = w_co[:, c * 128:(c + 1) * 128].rearrange("(k p) m -> p k m", p=128)
            dst = wimg[PMIX + c, :, 3072:3584].rearrange("p (k m) -> p k m", k=4)
            DM("pool", dst, src, [], [wpart(PMIX + c, "co")], ("grp", "pu"))

    prep_items = []
    prep_loaded = set()
    multi_keys = {"conv": [], "mix": []}

    prep_specs = []

    def scaled_item(W, kc, c0, gsel, stores):
        idx = len(prep_specs)
        prep_specs.append((W, kc, c0))

        def issue_load(k):
            if k >= len(prep_specs) or k in prep_loaded:
                return
            prep_loaded.add(k)
            W_, kc_, c0_ = prep_specs[k]
            i_ = k % 4
            DM("pool", wst[i_][:], W_[kc_ * 128:(kc_ + 1) * 128, c0_:c0_ + 1024], [], ["wst%d" % i_], ("wst", i_))

        def run():
            i = idx % 4
            issue_load(idx)
            issue_load(idx + 1)
            issue_load(idx + 2)
            issue_load(idx + 3)
            I("act", "activation", ["wst%d" % i, "gvec"], ["wbf%d" % i], out=wbf[i][:], in_=wst[i][:], func=AF.Copy,
              scale=gvec[:, gsel * 8 + kc: gsel * 8 + kc + 1])
            for n_, (sl, dst, pk) in enumerate(stores):
                DM("pool", dst, sl(wbf[i]), ["wbf%d" % i], [pk], ("wbf", i, n_))
        prep_items.append(run)

    def mkey(kind, kc, t_):
        k = ("wimg", kind, kc, t_)
        multi_keys[kind].append(k)
        return k

    def build_prep_items():
        for kc in range(8):
            scaled_item(w_in, kc, 0, 0, [
                (lambda t: t[:, 512:1024], img(PK, kc, 512, 0, 512), wpart(PK, kc)),
                (lambda t: t[:, 0:512], img(PQ, kc, 512, 0, 512), wpart(PQ, kc)),
            ])
            scaled_item(w_in, kc, 1024, 0, [
                (lambda t: t[:, 0:512], img(PV, kc, 512, 0, 512), wpart(PV, kc)),
                (lambda t: t[:, 512:1024].rearrange("p (c i) -> p c i", c=4),
                 wimg[PCONV:PCONV + 4, :, kc * 384: kc * 384 + 128].rearrange("c p i -> p c i"), mkey("conv", kc, 0)),
            ])
        for kc in range(8):
            scaled_item(w_in, kc, 2048, 0, [
                (lambda t: t[:, 0:512].rearrange("p (c i) -> p c i", c=4),
                 wimg[PCONV:PCONV + 4, :, kc * 384 + 128: kc * 384 + 256].rearrange("c p i -> p c i"), mkey("conv", kc, 1)),
                (lambda t: t[:, 512:1024].rearrange("p (c i) -> p c i", c=4),
                 wimg[PCONV:PCONV + 4, :, kc * 384 + 256: kc * 384 + 384].rearrange("c p i -> p c i"), mkey("conv", kc, 2)),
            ])
        for kc in range(8):
            for t_ in range(2):
                scaled_item(w_in, kc, 3072 + t_ * 1024, 0, [
                    (lambda t: t[:, :].rearrange("p (c i) -> p c i", c=8),
                     wimg[PMIX:PMIX + 8, :, kc * 256 + t_ * 128: kc * 256 + t_ * 128 + 128].rearrange("c p i -> p c i"), mkey("mix", kc, t_)),
                ])
        for kc in range(8):
            for s_ in range(4):
                scaled_item(w_up, kc, s_ * 1024, 1, [
                    (lambda t: t[:, 0:512], img(PUP + 2 * s_, kc, 512, 0, 512), wpart(PUP + 2 * s_, kc)),
                    (lambda t: t[:, 512:1024], img(PUP + 2 * s_ + 1, kc, 512, 0, 512), wpart(PUP + 2 * s_ + 1, kc)),
                ])
        for kc in range(8):
            scaled_item(w_pg, kc, 0, 2, [
                (lambda t: t[:, 0:512], img(PPG, kc, 512, 0, 512), wpart(PPG, kc)),
                (lambda t: t[:, 512:1024], img(PPG + 1, kc, 512, 0, 512), wpart(PPG + 1, kc)),
            ])

    class WS:
        def __init__(self):
            self.seq = []
            self.issued = 0
            self.used = 0

        def plan(self, pieces):
            self.seq.extend(pieces)

        def _issue(self):
            k = self.issued
            piece, ncol = self.seq[k]
            slot = k % 4
            rd = [("wimg", piece, p) for p in parts.get(piece, ())]
            if PCONV <= piece < PCONV + 4:
                rd += multi_keys["conv"]
            if PMIX <= piece < PMIX + 8:
                rd += multi_keys["mix"]
            DM("sp", wring[slot][:, 0:ncol], wimg[piece, :, 0:ncol], rd, ["wring%d" % slot], ("wring", slot))
            self.issued += 1

        def next(self):
            k = self.used
            while self.issued < min(len(self.seq), k + 4):
                self._issue()
            self.used += 1
            slot = k % 4
            return wring[slot], "wring%d" % slot

    ws = WS()

    def norm_stats(NS_, xr=xr, xk="xr"):
        for s in range(NS_):
            I("act", "activation", [(xk, s)], ["junk", "ss"], out=junk[:], in_=xr[:, s, :], func=AF.Square, accum_out=ss[:, s:s + 1])
        I("act", "activation", ["ss", "epsb"], ["sq"], out=sq[:, 0:NS_], in_=ss[:, 0:NS_], func=AF.Sqrt, bias=epsb[:, 0:1], scale=1.0 / D)
        I("dve", "reciprocal", ["sq"], ["rstd"], out=rstd[:, 0:NS_], in_=sq[:, 0:NS_])

    def norm_to_hT(NS_, NT_, hT=hT, hk="hT", xr=xr, xk="xr"):
        norm_stats(NS_, xr, xk)
        for s in range(NS_):
            I("dve", "tensor_scalar", [(xk, s), "rstd"], [("hbf", s)], out=hbf[:, s, :], in0=xr[:, s, :], scalar1=rstd[:, s:s + 1], scalar2=None, op0=ALU.mult)
        import os
        npair = int(os.environ.get("DBG_NPAIR", "4"))
        dbg_ev = os.environ.get("DBG_EV", "")
        for kp in range(npair):
            pt, pk = ptb()
            for kk in range(2):
                kc = kp * 2 + kk
                for s in range(NS_):
                    I("pe", "transpose", [("hbf", s), "ident"], [pk], out=pt[:, kk * 512 + s * 128: kk * 512 + (s + 1) * 128],
                      in_=hbf[:, s, kc * 128:(kc + 1) * 128], identity=ident[:])
            if dbg_ev == "none":
                continue
            copy_op(dbg_ev or evac_eng(), hT[:, kp * 2:kp * 2 + 2, 0:NT_], pt[:, :].rearrange("p (k t) -> p k t", k=2)[:, :, 0:NT_], [pk],
                    [(hk, kp * 2), (hk, kp * 2 + 1)])

    def mm_acc(bank_ap, bk, pairs, extra_reads):
        n = len(pairs)
        for i, (l, r, rk) in enumerate(pairs):
            I("pe", "matmul", list(rk) + list(extra_reads), [bk], out=bank_ap, lhsT=l, rhs=r, start=(i == 0), stop=(i == n - 1))

    def phase1a(kind, g):
        NT_ = 512 if kind == "P" else 128
        NS_ = NT_ // 128
        hT, hk = (globals_hT[0], "hT") if g % 2 == 0 else (mT, "mT")

        def stgr():
            i = rr_["sg"] % 4
            rr_["sg"] += 1
            return stg[i // 2][:, (i % 2) * 512:(i % 2 + 1) * 512], ("stgq", i)
        def xload(kind_, g_):
            xt, xk_ = (globals_xr[0], "xr") if (g_ % 2 == 0) else (xr2, "xr2")
            if (kind_, g_) in x_issued:
                return xt, xk_
            x_issued.add((kind_, g_))
            for s in range(4 if kind_ == "P" else 1):
                src = xall[g_ * 512 + s * 128: g_ * 512 + (s + 1) * 128, :] if kind_ == "P" else xsam[:, :]
                DM("sp", xt[:, s, :], src, [], [(xk_, s)], (xk_, s))
            return xt, xk_
        gi = g if kind == "P" else 16
        xr, xk = xload(kind, gi)
        if kind == "P":
            if g + 1 < NG:
                xload("P", g + 1)
            else:
                xload("S", 16)
        SUB = _SUB[0]
        import os
        dbs = int(os.environ.get("DBG_S", "9")) if kind == "S" else 9
        if SUB == 1:
            norm_stats(NS_, xr, xk)
            return
        norm_to_hT(NS_, NT_, hT, hk, xr, xk)
        if SUB == 2 or dbs == 1:
            return
        Wk, wkk = ws.next()
        Wk3 = Wk[:, :].rearrange("p (k c) -> p k c", k=8)
        if SUB == 3:
            return
        for h in range(8):
            bank, bk = psb()
            mm_acc(bank[0:64, 0:NT_], bk, [(Wk3[:, kc, h * 64:(h + 1) * 64], hT[:, kc, 0:NT_], [(hk, kc)]) for kc in range(8)], [wkk])
            copy_op(evac_eng(), attnT[0:64, h, 0:NT_], bank[0:64, 0:NT_], [bk], ["attnT"])
        if SUB == 4 or dbs == 2:
            return
        if kind == "P":
            DM("sp", KTp[:, :, g * 512:(g + 1) * 512].rearrange("h d t -> d h t"), attnT[0:64, :, :], ["attnT"], [("KTp", g)], "ktst")
        else:
            for i in range(4):
                DM("sp", KTs[i, :, :, 0:32].rearrange("h d t -> d h t"), attnT[0:64, :, i * 32:(i + 1) * 32], ["attnT"], [("KTs", i, "n")], ("ktst", i))
        if SUB == 5 or dbs == 3:
            return
        for s in range(NS_):
            bank, bk = psb()
            mm_acc(bank[:, :], bk, [(hT[:, kc, s * 128:(s + 1) * 128], Wk3[:, kc, :], [(hk, kc)]) for kc in range(8)], [wkk])
            sgt, sgk = stgr()
            copy_op(evac_eng(), sgt, bank[:, :], [bk], [sgk])
            dst = k_all[g * 512 + s * 128: g * 512 + (s + 1) * 128, :] if kind == "P" else k_s[:, :]
            DM("sp", dst, sgt, [sgk], [], sgk)
        if SUB == 6 or dbs == 4:
            return
        Wv, wvk = ws.next()
        Wv3 = Wv[:, :].rearrange("p (k c) -> p k c", k=8)
        for s in range(NS_):
            bank, bk = psb()
            mm_acc(bank[:, :], bk, [(hT[:, kc, s * 128:(s + 1) * 128], Wv3[:, kc, :], [(hk, kc)]) for kc in range(8)], [wvk])
            sgt, sgk = stgr()
            copy_op("act", sgt, bank[:, :], [bk], [sgk])
            copy_op("dve", ycb[:, s, :], sgt, [sgk], [("ycb", s)])
            dst = v_all[g * 512 + s * 128: g * 512 + (s + 1) * 128, :] if kind == "P" else v_s[:, :]
            DM("sp", dst, sgt, [sgk], [], sgk)
            if dbs == 5:
                continue
            if kind == "P":
                c = (g % 4) * 4 + s
                dstv = Vp[:, g // 4, :, c * 64:(c + 1) * 64].rearrange("h p d -> p h d")
                DM("sp", dstv, ycb[:, s, :].rearrange("p (h d) -> p h d", h=8), [("ycb", s)], [("Vp", g // 4, g % 4, s)], ("vst", s))
            else:
                for i in range(4):
                    DM("sp", Vsn[i].rearrange("h t d -> t h d"), ycb[i * 32:(i + 1) * 32, 0, :].rearrange("p (h d) -> p h d", h=8),
                       [("ycb", 0)], [("Vsn", i)], ("vsn", i))

    def cache_prep():
        for i in range(4):
            for hf in range(2):
                DM("pool", Vs[i, hf * 1024:(hf + 1) * 1024, :], cv[i, hf * 1024:(hf + 1) * 1024, :], [], [("Vs", i, hf)], ("grp", "cvs"))
        n = 0
        for i in range(4):
            for blk in range(4):
                b = n % 2
                n += 1
                DM("pool", ckb[b][:], ck[i, blk * 512:(blk + 1) * 512, :].rearrange("(c p) f -> p c f", p=128), [], ["ckb%d" % b], ("ckb", b))
                for hp in range(4):
                    pt, pk = ptb()
                    for hh in range(2):
                        h = hp * 2 + hh
                        for c in range(4):
                            I("pe", "transpose", ["ckb%d" % b, "ident"], [pk], out=pt[0:64, hh * 512 + c * 128: hh * 512 + (c + 1) * 128],
                              in_=ckb[b][:, c, h * 64:(h + 1) * 64], identity=ident[:])
                    copy_op(evac_eng(), attnT[0:64, hp * 2:hp * 2 + 2, :], pt[0:64, :].rearrange("p (k t) -> p k t", k=2), [pk], ["attnT"])
                DM("sp", KTs[i, :, :, 32 + blk * 512: 32 + (blk + 1) * 512].rearrange("h d t -> d h t"), attnT[0:64, :, :],
                   ["attnT"], [("KTs", i, blk)], "ktst")

    def attention(kind, j):
        chains = []
        if kind == "P":
            kb0 = 2 * j
            jpar = j % 2
            for h in range(8):
                blocks = []
                for i, kb in enumerate(range(kb0, 16)):
                    if i == 0:
                        mk = [mbt[:, (jpar * 2 + 0) * 4 + r, :] for r in range(4)]
                    elif i == 1:
                        mk = [mbt[:, (jpar * 2 + 1) * 4 + r, :] for r in range(4)]
                    else:
                        mk = [zmask[:, :]] * 4
                    blocks.append(dict(nk=512, src=("P", h, kb // 4), sub=kb % 4, mask=mk))
                chains.append(dict(h=h, q0=0, R=4, nq=128, blocks=blocks))
        else:
            for i in range(4):
                for h in range(8):
                    blocks = [dict(nk=32, src=("N", i, h), sub=0, mask=[mbt[0:32, 16, 0:32]])]
                    for blk in range(4):
                        blocks.append(dict(nk=512, src=("S", i, h), sub=blk, mask=[zmask[0:32, :]]))
                    chains.append(dict(h=h, q0=i * 32, R=1, nq=32, blocks=blocks))
        steps = [(ci, bi) for ci, ch in enumerate(chains) for bi in range(len(ch["blocks"]))]
        cur_src = [None, None]
        pending = None
        src_order = []
        for ch in chains:
            for blk in ch["blocks"]:
                if blk["src"][0] != "N" and blk["src"] not in src_order:
                    src_order.append(blk["src"])
        kv_slot = {}

        def issue_kv(src):
            slot = rr_["kv"] % 2
            rr_["kv"] += 1
            kv_slot[src] = slot
            V3d = Vr[slot][:, :].rearrange("p (c d) -> p c d", c=16)
            D3d = Vd[slot][:, :].rearrange("p (c d) -> p c d", c=16)
            if src[0] == "P":
                _, h_, rb = src
                DM("sp", KTr[slot][:, :], KTp[h_, :, rb * 2048:(rb + 1) * 2048],
                   [("KTp", g_) for g_ in range(rb * 4, rb * 4 + 4)], ["KTr%d" % slot], ("KTr", slot))
                DM("sp", Vr[slot][:, :], Vp[h_, rb],
                   [("Vp", rb, a, b_) for a in range(4) for b_ in range(4)], ["Vr%d" % slot], ("Vr", slot))
                vp_rd = [("Vp", rb, a, b_) for a in range(4) for b_ in range(4)]
                DM("sp", Vd[slot][0:127, :], Vp[h_, rb, 1:128, :], vp_rd, ["Vd%d" % slot, ("Vd%d" % slot, 1)], ("Vd", slot))
                DM("sp", Vd[slot][127:128, 0:960], Vp[h_, rb, 0:1, 64:1024], vp_rd, [("Vd%d" % slot, 2)], ("Vd", slot))
                if rb < 3:
                    DM("sp", Vd[slot][127:128, 960:1024], Vp[h_, rb + 1, 0:1, 0:64], [("Vp", rb + 1, 0, 0)], [("Vd%d" % slot, 3)], ("Vd", slot))
                else:
                    DM("sp", Vd[slot][127:128, 960:1024], zrow[:, 0:64], ["zrow"], [("Vd%d" % slot, 4)], ("Vd", slot))
            else:
                _, i_, h_ = src
                DM("sp", KTr[slot][:, :], KTs[i_, h_, :, 32:32 + PAST],
                   [("KTs", i_, b_) for b_ in range(4)], ["KTr%d" % slot], ("KTr", slot))
                for hf in range(2):
                    DM("sp", V3d[:, hf * 8:(hf + 1) * 8, :],
                       Vs[i_, hf * 1024:(hf + 1) * 1024, h_ * 64:(h_ + 1) * 64].rearrange("(c p) d -> p c d", p=128),
                       [("Vs", i_, hf)], ["Vr%d" % slot], ("Vr", slot))
                DM("sp", D3d[:, 0:8, :], Vs[i_, 1:1025, h_ * 64:(h_ + 1) * 64].rearrange("(c p) d -> p c d", p=128),
                   [("Vs", i_, 0), ("Vs", i_, 1)], ["Vd%d" % slot, ("Vd%d" % slot, 5)], ("Vd", slot))
                DM("sp", D3d[:, 8:15, :], Vs[i_, 1025:1921, h_ * 64:(h_ + 1) * 64].rearrange("(c p) d -> p c d", p=128),
                   [("Vs", i_, 1)], [("Vd%d" % slot, 6)], ("Vd", slot))
                DM("sp", D3d[0:127, 15, :], Vs[i_, 1921:2048, h_ * 64:(h_ + 1) * 64], [("Vs", i_, 1)], [("Vd%d" % slot, 7)], ("Vd", slot))
                DM("sp", D3d[127:128, 15, :], zrow[:, 0:64], ["zrow"], [("Vd%d" % slot, 8)], ("Vd", slot))
            I("pool", "tensor_tensor", ["Vr%d" % slot] + [("Vd%d" % slot, n_) for n_ in range(1, 10)], ["Vd%d" % slot] + [("Vd%d" % slot, n_) for n_ in range(1, 10)], out=Vd[slot][:, :], in0=Vd[slot][:, :], in1=Vr[slot][:, :], op=ALU.subtract)

        pendingC = None
        pf_queue = []
        for st in range(len(steps) + 2):
            newp = None
            want_prefetch = None
            if st < len(steps):
                ci, bi = steps[st]
                ch = chains[ci]
                blk = ch["blocks"][bi]
                R, nq, nk, h = ch["R"], ch["nq"], blk["nk"], ch["h"]
                rbase = (ci % 4) if kind == "S" else 0
                src = blk["src"]
                if src[0] == "N":
                    _, i_, h_ = src
                    DM("sp", KTn[:, :], KTs[i_, h_, :, 0:32], [("KTs", i_, "n")], ["KTn"], "KTn")
                    DM("sp", Vn[:, :], Vsn[i_, h_], [("Vsn", i_)], ["Vn"], "Vn")
                    kt_ap = KTn[0:64, 0:32]
                    ktk, vk = "KTn", "Vn"
                    vch = [(Vn[0:32, :], 32)]
                    abel = False
                    vrow0 = None
                else:
                    if src not in kv_slot:
                        issue_kv(src)
                    k_ = src_order.index(src)
                    if k_ + 1 < len(src_order) and src_order[k_ + 1] not in kv_slot:
                        want_prefetch = src_order[k_ + 1]
                    cur_src = [src, kv_slot[src]]
                    slot = cur_src[1]
                    sub = blk["sub"]
                    kt_ap = KTr[slot][0:64, sub * 512:(sub + 1) * 512]
                    ktk, vk = "KTr%d" % slot, "Vr%d" % slot
                    V3 = Vr[slot][:, :].rearrange("p (c d) -> p c d", c=16)
                    D3 = Vd[slot][:, :].rearrange("p (c d) -> p c d", c=16)
                    abel = bi >= 2
                    if abel:
                        vch = [(D3[:, sub * 4 + c, :], 128) for c in range(4)]
                        vk = "Vd%d" % slot
                    else:
                        vch = [(V3[:, sub * 4 + c, :], 128) for c in range(4)]
                    vrow0 = (V3[0:1, sub * 4, :], "Vr%d" % slot)
                s_ = bi % 2
                ic = 512 * s_ + 512 - nk
                nblk = len(ch["blocks"])
                abufs = []
                casts = []
                for r in range(R):
                    rr = (rbase + r) % 4
                    cpt, cpk = cp[rr], "cp%d" % rr
                    if bi == 0:
                        I("dve", "memset", [], [(cpk, "c", s_), (cpk, "s", s_)], ap=cpt[:, ic:ic + 1], constant=1.0)
                    bank, bk = PS[rr], "ps%d" % rr
                    qa = qT[0:64, h, ch["q0"] + r * nq: ch["q0"] + (r + 1) * nq]
                    I("pe", "matmul", ["qT", ktk], [bk], out=bank[0:nq, 0:nk], lhsT=qa, rhs=kt_ap, start=True, stop=True)
                    oi = rr_["om"] % 4
                    rr_["om"] += 1
                    omt, omk = om[oi], "om%d" % oi
                    I("act", "activation", [bk], [omk], out=omt[0:nq, 0:nk], in_=bank[0:nq, 0:nk], func=AF.Sigmoid, scale=-0.125)
                    rdc = [(cpk, "c", s_)] if (bi == 0 or s_ == 0) else [(cpk, "s", 1 - s_)]
                    I("dve", "tensor_tensor_scan", [omk, "mbt", "zmask"] + rdc, [(cpk, "s", s_)],
                      out=cpt[0:nq, ic + 1: ic + 1 + nk], data0=omt[0:nq, 0:nk], data1=blk["mask"][r], initial=cpt[0:nq, ic:ic + 1],
                      op0=ALU.mult, op1=ALU.max)
                    ai = s_ * 4 + rr
                    at_, ak = Ab[ai], "Ab%d" % ai
                    if abel:
                        casts.append((cpk, s_, at_, ak, cpt[0:nq, ic + 1: ic + 1 + nk], at_[0:nq, 0:nk]))
                    else:
                        I("dve", "tensor_tensor", [(cpk, "s", s_)] + rdc, [ak], out=at_[0:nq, 0:nk], in0=cpt[0:nq, ic: ic + nk],
                          in1=cpt[0:nq, ic + 1: ic + 1 + nk], op=ALU.subtract)
                        if bi == 1 and nblk > 2:
                            I("dve", "tensor_copy", [(cpk, "s", s_), ak], [ak], out=at_[0:nq, 512:513], in_=cpt[0:nq, ic + nk: ic + nk + 1])
                    if s_ == 1 and bi + 1 < nblk:
                        I("pool", "tensor_copy", [(cpk, "s", 1)], [(cpk, "c", 0)], out=cpt[0:nq, 0:1], in_=cpt[0:nq, 1024:1025])
                    abufs.append((at_, ak))
                newp = dict(q0=ch["q0"], bi=bi, nblk=nblk, abufs=abufs, vch=vch, vk=vk, R=R, nq=nq, nk=nk, h=h, ci=ci, abel=abel,
                            vrow0=vrow0, casts=casts)
                if pending is not None and pending["bi"] == 1 and pending["nblk"] > 2 and pending["ci"] == ci:
                    pending["next_vrow0"] = vrow0
            if pendingC is not None:
                for (rd_, wk_, kw_) in pendingC["mml"]:
                    I("pe", "matmul", rd_, [wk_], **kw_)
                if pendingC["fin"] is not None:
                    o_, i_ap, k_ = pendingC["fin"]
                    copy_op("dve", o_, i_ap, [k_], ["attnT"])
            newC = None
            if pending is not None:
                pd = pending
                R2, nq2, h2 = pd["R"], pd["nq"], pd["h"]
                acc, acck = (PS[4], "ps4") if pd["ci"] % 2 == 0 else (PS[5], "ps5")
                ncs = len(pd["vch"])
                mml = []
                for (cpk_, s__, at__, ak_, src_ap, dst_ap) in pd["casts"]:
                    I("act", "copy", [(cpk_, "s", s__)], [ak_], out=dst_ap, in_=src_ap)
                for cp_ in range((ncs + 1) // 2):
                    pt, pk = ptb()
                    cl = [c for c in (2 * cp_, 2 * cp_ + 1) if c < ncs]
                    for ci_, c in enumerate(cl):
                        va, kp = pd["vch"][c]
                        for r in range(R2):
                            at_, ak = pd["abufs"][r]
                            I("pe", "transpose", [ak, "ident"], [pk], out=pt[0:kp, ci_ * 512 + r * nq2: ci_ * 512 + (r + 1) * nq2],
                              in_=at_[0:nq2, c * 128: c * 128 + kp], identity=ident[0:nq2, 0:nq2])
                    ati = rr_["at"] % 4
                    rr_["at"] += 1
                    att, atk = ATb[ati], "ATb%d" % ati
                    kp0 = pd["vch"][cl[0]][1]
                    ev_e = "dve" if (pd["abel"] and cp_ == 1 and pd["bi"] % 2 == 0) else "act"
                    if len(cl) == 2:
                        copy_op(ev_e, att[0:kp0, :].rearrange("p (k t) -> p k t", k=2)[:, :, 0:R2 * nq2],
                                pt[0:kp0, :].rearrange("p (k t) -> p k t", k=2)[:, :, 0:R2 * nq2], [pk], [atk])
                    else:
                        copy_op("act", att[0:kp0, 0:R2 * nq2], pt[0:kp0, 0:R2 * nq2], [pk], [atk])
                    for ci_, c in enumerate(cl):
                        va, kp = pd["vch"][c]
                        first = (pd["bi"] == 0 and c == 0)
                        last = (pd["bi"] == pd["nblk"] - 1 and c == ncs - 1)
                        mml.append(([pd["vk"], atk], acck, dict(out=acc[0:64, 0:R2 * nq2], lhsT=va, rhs=att[0:kp, ci_ * 512: ci_ * 512 + R2 * nq2], start=first, stop=last)))
                if pd.get("next_vrow0") is not None:
                    vr_ap, vr_k = pd["next_vrow0"]
                    pt, pk = ptb()
                    for r in range(R2):
                        at_, ak = pd["abufs"][r]
                        I("pe", "transpose", [ak, "ident"], [pk], out=pt[0:1, r * nq2:(r + 1) * nq2], in_=at_[0:nq2, 512:513], identity=ident[0:nq2, 0:nq2])
                    ati = rr_["at"] % 4
                    rr_["at"] += 1
                    att, atk = ATb[ati], "ATb%d" % ati
                    copy_op("act", att[0:1, 0:R2 * nq2], pt[0:1, 0:R2 * nq2], [pk], [atk])
                    mml.append(([vr_k, atk], acck, dict(out=acc[0:64, 0:R2 * nq2], lhsT=vr_ap, rhs=att[0:1, 0:R2 * nq2], start=False, stop=False)))
                fin = None
                if pd["bi"] == pd["nblk"] - 1:
                    q0 = pd["q0"]
                    fin = (attnT[0:64, h2, q0: q0 + R2 * nq2], acc[0:64, 0:R2 * nq2], acck)
                newC = dict(mml=mml, fin=fin)
            pendingC = newC
            pending = newp
            if want_prefetch is not None:
                pf_queue.append((st + 2, want_prefetch))
            while pf_queue and pf_queue[0][0] <= st:
                _, src_ = pf_queue.pop(0)
                if src_ not in kv_slot:
                    issue_kv(src_)

    def group(kind, j):
        NT_ = 512 if kind == "P" else 128
        NS_ = NT_ // 128
        L = 512 if kind == "P" else 32

        def uev(t, a, b):
            if kind == "P":
                return t[:, a:b]
            return t[:, 0:136].rearrange("p (s l) -> p s l", s=4)[:, :, a:b]

        def v3(ap2):
            if kind == "P":
                return ap2
            return ap2.rearrange("p (s l) -> p s l", s=4)
        for s in range(NS_):
            src = xown[j, s * 128:(s + 1) * 128, :] if kind == "P" else xsam[:, :]
            DM("sp", xr[:, s, :], src, [], [("xr", s)], ("xr", s))
        norm_to_hT(NS_, NT_)
        Wq, wqk = ws.next()
        Wq3 = Wq[:, :].rearrange("p (k c) -> p k c", k=8)
        for h in range(8):
            bank, bk = psb()
            mm_acc(bank[0:64, 0:NT_], bk, [(Wq3[:, kc, h * 64:(h + 1) * 64], hT[:, kc, 0:NT_], [("hT", kc)]) for kc in range(8)], [wqk])
            copy_op(evac_eng(), qT[0:64, h, 0:NT_], bank[0:64, 0:NT_], [bk], ["qT"])
        hTh3 = hTh[:, :].rearrange("p (k t) -> p k t", k=8)
        if kind == "P":
            DM("sp", xh[:, :], xhalo[j], [], ["xh"], "xh")
            I("act", "activation", ["xh"], ["junk", "ssh"], out=junk[0:2, :], in_=xh[:, :], func=AF.Square, accum_out=ssh[0:2, :])
            I("act", "activation", ["ssh", "epsb"], ["sqh"], out=sqh[0:2, :], in_=ssh[0:2, :], func=AF.Sqrt, bias=epsb[0:2, 0:1], scale=1.0 / D)
            I("dve", "reciprocal", ["sqh"], ["rsh"], out=rsh[0:2, :], in_=sqh[0:2, :])
            I("dve", "tensor_scalar", ["xh", "rsh"], ["hhb"], out=hhb[:, :], in0=xh[:, :], scalar1=rsh[0:2, 0:1], scalar2=None, op0=ALU.mult)
            pt, pk = ptb()
            for kc in range(8):
                I("pe", "transpose", ["hhb", "ident"], [pk], out=pt[:, kc * 2: kc * 2 + 2], in_=hhb[0:2, kc * 128:(kc + 1) * 128], identity=ident[0:2, 0:2])
            copy_op("dve", hTh[:, :], pt[:, 0:16], [pk], ["hTh"])
        for c in range(4):
            Wc, wck = ws.next()
            Wc3 = Wc[:, 0:3072].rearrange("p (k c) -> p k c", k=8)
            banks = [psb() for _ in range(3)]
            for t_ in range(3):
                mm_acc(banks[t_][0][:, 0:NT_], banks[t_][1], [(Wc3[:, kc, t_ * 128:(t_ + 1) * 128], hT[:, kc, 0:NT_], [("hT", kc)]) for kc in range(8)], [wck])
            i2 = c % 2
            uet, uek = ue[i2], "ue%d" % i2
            if kind == "P":
                bh, bhk = psb()
                for t_ in (1, 2):
                    mm_acc(bh[:, (t_ - 1) * 2:(t_ - 1) * 2 + 2], bhk, [(Wc3[:, kc, t_ * 128:(t_ + 1) * 128], hTh3[:, kc, :], ["hTh"]) for kc in range(8)], [wck])
                I("act", "copy", [bhk], ["cxh"], out=cxh[:, :], in_=bh[:, 2:4])
                I("dve", "tensor_tensor", [bhk, "cxh"], [(uek, "h")], out=uet[:, 512:514], in0=bh[:, 0:2], in1=cxh[:, :], op=ALU.mult)
            else:
                DM("sp", uev(uet, 32, 34), cconvT[c * 128:(c + 1) * 128, :].rearrange("p (s t) -> p s t", s=4), [], [(uek, "h")], ("ueh", i2))
            (bcb, bcbk), (bcc, bcck), (bcx, bcxk) = banks
            I("act", "copy", [bcxk], ["cxs%d" % i2], out=cxs[i2][:, 0:NT_], in_=bcx[:, 0:NT_])
            I("dve", "tensor_tensor", [bcck, "cxs%d" % i2], [(uek, "m")], out=uev(uet, 0, L), in0=v3(bcc[:, 0:NT_]), in1=v3(cxs[i2][:, 0:NT_]), op=ALU.mult)
            if kind == "S":
                DM("sp", conv_s[:, c, :, :], uev(uet, 0, 2), [(uek, "m")], [], ("ueo", i2))
            elif j == 0:
                DM("sp", conv_p[:, c, :], uet[:, 0:2], [(uek, "m")], [], ("ueo", i2))
            yt, yk = yac[i2], "yac%d" % i2
            I("dve", "tensor_scalar", [(uek, "m"), "cw"], [yk], out=v3(yt[:, 0:NT_]), in0=uev(uet, 0, L), scalar1=cw[:, c, 2:3], scalar2=None, op0=ALU.mult)
            for (sh, wi) in ((1, 1), (2, 0)):
                I("dve", "scalar_tensor_tensor", [(uek, "m"), (uek, "h"), "cw", yk], [yk], out=v3(yt[:, 0:NT_]), in0=uev(uet, sh, L + sh),
                  scalar=cw[:, c, wi:wi + 1], in1=v3(yt[:, 0:NT_]), op0=ALU.mult, op1=ALU.add)
            I("dve", "tensor_tensor", [bcbk, yk], [("ycb", c)], out=ycb[:, c, 0:NT_], in0=bcb[:, 0:NT_], in1=yt[:, 0:NT_], op=ALU.mult)
        attention(kind, j)
        for c in range(8):
            Wm, wmk = ws.next()
            Wg3 = Wm[:, 0:2048].rearrange("p (k c) -> p k c", k=8)
            Wa3 = Wm[0:64, 2048:3072].rearrange("p (h m) -> p h m", h=8)
            Wo3 = Wm[:, 3072:3584].rearrange("p (k m) -> p k m", k=4)
            (bga, bgak), (bgc, bgck), (bya, byak), (byc, byck) = [psb() for _ in range(4)]
            mm_acc(bga[:, 0:NT_], bgak, [(Wg3[:, kc, 0:128], hT[:, kc, 0:NT_], [("hT", kc)]) for kc in range(8)], [wmk])
            mm_acc(bgc[:, 0:NT_], bgck, [(Wg3[:, kc, 128:256], hT[:, kc, 0:NT_], [("hT", kc)]) for kc in range(8)], [wmk])
            mm_acc(bya[:, 0:NT_], byak, [(Wa3[:, h, :], attnT[0:64, h, 0:NT_], ["attnT"]) for h in range(8)], [wmk])
            mm_acc(byc[:, 0:NT_], byck, [(Wo3[:, kc, :], ycb[:, kc, 0:NT_], [("ycb", kc)]) for kc in range(4)], [wmk])
            i2 = c % 2
            I("act", "activation", [bgak], ["cxs%d" % i2], out=sga[i2][:, 0:NT_], in_=bga[:, 0:NT_], func=AF.Sigmoid)
            I("act", "activation", [bgck], ["yac%d" % i2], out=sgc[i2][:, 0:NT_], in_=bgc[:, 0:NT_], func=AF.Sigmoid)
            I("dve", "tensor_tensor", [byak, "cxs%d" % i2], ["t1_%d" % i2], out=t1[i2][:, 0:NT_], in0=bya[:, 0:NT_], in1=sga[i2][:, 0:NT_], op=ALU.mult)
            I("dve", "tensor_tensor", [byck, "yac%d" % i2], ["yac%d" % i2], out=sgc[i2][:, 0:NT_], in0=byc[:, 0:NT_], in1=sgc[i2][:, 0:NT_], op=ALU.mult)
            I("dve", "tensor_tensor", ["t1_%d" % i2, "yac%d" % i2], [("mT", c)], out=mT[:, c, 0:NT_], in0=t1[i2][:, 0:NT_], in1=sgc[i2][:, 0:NT_], op=ALU.add)
        for jc in range(2):
            Wo_, wok = ws.next()
            W3 = Wo_[:, :].rearrange("p (k c) -> p k c", k=8)
            for s in range(NS_):
                bank, bk = psb()
                mm_acc(bank[:, :], bk, [(mT[:, kc, s * 128:(s + 1) * 128], W3[:, kc, :], [("mT", kc)]) for kc in range(8)], [wok])
                xs_ = xr[:, s, jc * 512:(jc + 1) * 512]
                I("dve", "tensor_tensor", [bk, ("xr", s)], [("xr", s)], out=xs_, in0=xs_, in1=bank[:, :], op=ALU.add)
        norm_to_hT(NS_, NT_)
        for pu in range(8):
            Wu, wuk = ws.next()
            W3 = Wu[:, :].rearrange("p (k c) -> p k c", k=8)
            for m in range(4):
                bank, bk = psb()
                mm_acc(bank[:, 0:NT_], bk, [(W3[:, kc, m * 128:(m + 1) * 128], hT[:, kc, 0:NT_], [("hT", kc)]) for kc in range(8)], [wuk])
                i2 = m % 2
                I("act", "activation", [bk], ["t1_%d" % i2], out=t1[i2][:, 0:NT_], in_=bank[:, 0:NT_], func=AF.Relu)
                wr = [("fT", pu * 4 + m)] + (REGB + ["regB_tok"] if (pu == 0 and m == 0) else [])
                I("pool", "tensor_tensor", ["t1_%d" % i2], wr, out=fT[:, pu * 4 + m, 0:NT_], in0=t1[i2][:, 0:NT_], in1=t1[i2][:, 0:NT_], op=ALU.mult)
        for chh in range(2):
            bks = [psb() for _ in range(NS_)]
            for kq in range(4):
                Wd, wdk = ws.next()
                W3 = Wd[:, :].rearrange("p (k c) -> p k c", k=8)
                for s in range(NS_):
                    bank, bk = bks[s]
                    for kc in range(8):
                        I("pe", "matmul", [("fT", kq * 8 + kc), wdk], [bk], out=bank[:, :], lhsT=fT[:, kq * 8 + kc, s * 128:(s + 1) * 128], rhs=W3[:, kc, :],
                          start=(kq == 0 and kc == 0), stop=(kq == 3 and kc == 7))
            for s in range(NS_):
                bank, bk = bks[s]
                xs_ = xr[:, s, chh * 512:(chh + 1) * 512]
                I("dve", "tensor_tensor", [bk, ("xr", s)], [("xr", s)], out=xs_, in0=xs_, in1=bank[:, :], op=ALU.add)
        norm_to_hT(NS_, NT_)
        psrc = pown[j].rearrange("(s p) f -> p s f", p=128) if kind == "P" else psam.rearrange("(s p) f -> p s f", p=128)
        DM("sp", pst[:, 0:NS_, :], psrc, ["regB_tok"], ["pst"], "pst")
        I("dve", "tensor_copy", ["pst"], ["pbf"], out=pbf[:, 0:NS_, :], in_=pst[:, 0:NS_, :])
        pt, pk = ptb()
        for k2 in range(2):
            for s in range(NS_):
                I("pe", "transpose", ["pbf", "ident"], [pk], out=pt[:, k2 * 512 + s * 128: k2 * 512 + (s + 1) * 128], in_=pbf[:, s, k2 * 128:(k2 + 1) * 128], identity=ident[:])
        copy_op(evac_eng(), pT[:, 0:2, 0:NT_], pt[:, :].rearrange("p (k t) -> p k t", k=2)[:, :, 0:NT_], [pk], [("pT", 0), ("pT", 1)])
        for jc in range(2):
            Wg, wgk = ws.next()
            W3 = Wg[:, :].rearrange("p (k c) -> p k c", k=8)
            for s in range(NS_):
                bank, bk = psb()
                mm_acc(bank[:, :], bk, [(hT[:, kc, s * 128:(s + 1) * 128], W3[:, kc, :], [("hT", kc)]) for kc in range(8)], [wgk])
                I("act", "activation", [bk], [("gate", s)], out=gate[:, s, :], in_=bank[:, :], func=AF.Sigmoid)
            Wl, wlk = ws.next()
            Wl3 = Wl[:, 0:1024].rearrange("p (k c) -> p k c", k=2)
            for s in range(NS_):
                bank, bk = psb()
                mm_acc(bank[:, :], bk, [(pT[:, k2, s * 128:(s + 1) * 128], Wl3[:, k2, :], [("pT", k2)]) for k2 in range(2)], [wlk])
                I("dve", "tensor_tensor", [bk, ("gate", s)], [("gate", s)], out=gate[:, s, :], in0=bank[:, :], in1=gate[:, s, :], op=ALU.mult)
                xs_ = xr[:, s, jc * 512:(jc + 1) * 512]
                I("dve", "tensor_tensor", [("gate", s), ("xr", s)], [("xr", s)], out=xs_, in0=xs_, in1=gate[:, s, :], op=ALU.add)
        norm_stats(NS_)
        for s in range(NS_):
            i = s % 2
            I("dve", "scalar_tensor_tensor", [("xr", s), "rstd", "gfin"], ["stg%d" % i, ("stgq", 2 * i), ("stgq", 2 * i + 1)], out=stg[i][:, :], in0=xr[:, s, :], scalar=rstd[:, s:s + 1],
              in1=gfin[:, :], op0=ALU.mult, op1=ALU.mult)
            dst = y_own[j, s * 128:(s + 1) * 128, :] if kind == "P" else y_s[:, :]
            DM("sp", dst, stg[i][:, :], ["stg%d" % i], [], ("stg", i))
        I("pool", "memset", [], [("fT", i) for i in range(32)] + ["pst", "pbf", ("pT", 0), ("pT", 1)] + [("gate", s) for s in range(4)] + REGB,
          ap=cp[0][:, 0:1], constant=1.0)

    grp_pieces = [(PQ, 4096)] + [(PCONV + c, 3072) for c in range(4)] + [(PMIX + c, 3584) for c in range(8)] + \
                 [(PO, 4096), (PO + 1, 4096)] + [(PUP + i, 4096) for i in range(8)] + [(PDN + i, 4096) for i in range(8)] + \
                 [(PPG, 4096), (PPLE, 1024), (PPG + 1, 4096), (PPLE + 1, 1024)]
    for _ in range(17):
        ws.plan([(PK, 4096), (PV, 4096)])
    for _ in range(9):
        ws.plan(grp_pieces)

    prep_unscaled()
    build_prep_items()
    pi = [0]

    def run_prep(n):
        for _ in range(n):
            if pi[0] < len(prep_items):
                prep_items[pi[0]]()
                pi[0] += 1
    STAGE = _STAGE[0]
    run_prep(16)
    if STAGE >= 2:
        for g in range(NG if _SUB[0] == 0 else _NGRP[0]):
            phase1a("P", g)
            run_prep(6)
        if _SUB[0] in (0, 8):
            phase1a("S", 0)
    run_prep(len(prep_items))
    if STAGE >= 3:
        cache_prep()
    I("pool", "memset", [], ["wst%d" % i_ for i_ in range(4)] + ["wbf%d" % i_ for i_ in range(4)] + ["ckb0", "ckb1"] + [("xr2", s_) for s_ in range(4)] + REGB, ap=cp[0][:, 0:1], constant=1.0)
    if STAGE >= 4:
        for j in range(8 if STAGE >= 5 else 1):
            group("P", j)
    if STAGE >= 6:
        group("S", 0)
    P.emit()
    print("instructions recorded:", len(P.ins))
    return nc


def _fix_none_keys():
    pass


_NC = [None]


def kernel(x_prompt, x_sample, p_prompt, p_sample, cache_k, cache_v, cache_conv,
           g_mix, w_in, conv_w, w_attn_out, w_conv_out, w_o,
           g_ffn, w_up, w_down, g_ple, w_ple_gate, w_ple, g_final):
    f = lambda a: np.ascontiguousarray(np.asarray(a, dtype=np.float32))
    x_prompt, x_sample, p_prompt, p_sample = f(x_prompt), f(x_sample), f(p_prompt), f(p_sample)
    cache_k, cache_v, cache_conv = f(cache_k), f(cache_v), f(cache_conv)
    if _NC[0] is None:
        _NC[0] = build_program()
    nc = _NC[0]
    diag = np.zeros((4, 128, 512), np.float32)
    for r in range(4):
        diag[r] = (np.arange(512)[None, :] <= (128 * r + np.arange(128))[:, None]).astype(np.float32)
    ones = np.ones((128, 512), np.float32)
    zeros = np.zeros((128, 512), np.float32)
    gvec = np.stack([np.asarray(g, np.float32).reshape(8, 128).T for g in (g_mix[0], g_ffn[0], g_ple[0])], axis=1).reshape(128, 24)
    shared = dict(
        gfin=f(np.broadcast_to(np.asarray(g_final, np.float32), (128, D))),
        convwT=f(np.asarray(conv_w[0], np.float32).T),
        gvec=f(gvec),
        w_in=f(w_in[0]), w_ao=f(w_attn_out[0]), w_co=f(w_conv_out[0]), w_o=f(w_o[0]),
        w_up=f(w_up[0]), w_dn=f(w_down[0]), w_pg=f(w_ple_gate[0]), w_ple=f(w_ple[0]),
    )
    in_maps = []
    for c in range(8):
        b, p = c // 2, c % 2
        xs = x_prompt[b, ::-1]
        ps = p_prompt[0, b, ::-1]
        xown = np.stack([xs[512 * g:512 * g + 512] for g in GRP[p]])
        pown = np.stack([ps[512 * g:512 * g + 512] for g in GRP[p]])
        xhalo = np.zeros((8, 2, D), np.float32)
        for j, g in enumerate(GRP[p]):
            if g < 15:
                xhalo[j] = xs[512 * (g + 1): 512 * (g + 1) + 2]
        mbank = np.zeros((17, 128, 512), np.float32)
        for jpar in range(2):
            caseA = (p == 0 and jpar == 0) or (p == 1 and jpar == 1)
            for r in range(4):
                mbank[(jpar * 2 + 0) * 4 + r] = diag[r] if caseA else ones
                mbank[(jpar * 2 + 1) * 4 + r] = zeros if caseA else diag[r]
        mbank[16] = diag[0]
        sl = slice(4 * c, 4 * c + 4)
        xsam = x_sample[sl, ::-1].reshape(128, D)
        psam = p_sample[0, sl, ::-1].reshape(128, 256)
        ckc = cache_k[0, sl, ::-1].reshape(4, PAST, 512)
        cvc = cache_v[0, sl, ::-1].reshape(4, PAST, 512)
        cc = cache_conv[0, sl, ::-1]
        cconvT = np.transpose(cc, (2, 0, 1)).reshape(512, 8)
        m = dict(xall=f(xs), xown=f(xown), xhalo=f(xhalo), pown=f(pown), xsam=f(xsam), psam=f(psam),
                 ck=f(ckc), cv=f(cvc), cconvT=f(cconvT), mb=f(mbank))
        m.update(shared)
        in_maps.append(m)
    ncr = _NCORES[0]
    res = run_bass_kernel_spmd(nc, in_maps[:ncr], core_ids=list(range(ncr)))
    R = list(res.results) + [res.results[0]] * (8 - ncr)
    y_prompt = np.zeros((4, T, D), np.float32)
    k_prompt = np.zeros((1, 4, T, 8, 64), np.float32)
    v_prompt = np.zeros((1, 4, T, 8, 64), np.float32)
    conv_prompt = np.zeros((1, 4, 2, 512), np.float32)
    y_sample = np.zeros((32, 32, D), np.float32)
    k_sample = np.zeros((1, 32, 32, 8, 64), np.float32)
    v_sample = np.zeros((1, 32, 32, 8, 64), np.float32)
    conv_sample = np.zeros((1, 32, 2, 512), np.float32)
    for c in range(8):
        b, p = c // 2, c % 2
        r = R[c]
        ys = np.zeros((T, D), np.float32) if p == 0 else None
        for j, g in enumerate(GRP[p]):
            y_prompt[b, T - 512 * (g + 1): T - 512 * g] = np.asarray(r["y_own"][j])[::-1]
        if p == 0:
            k_prompt[0, b] = np.asarray(r["k_all"])[::-1].reshape(T, 8, 64)
            v_prompt[0, b] = np.asarray(r["v_all"])[::-1].reshape(T, 8, 64)
            cp_ = np.asarray(r["conv_p"])
            u = np.transpose(cp_, (1, 0, 2)).reshape(512, 2)
            conv_prompt[0, b, 0] = u[:, 1]
            conv_prompt[0, b, 1] = u[:, 0]
        sl = slice(4 * c, 4 * c + 4)
        y_sample[sl] = np.asarray(r["y_s"]).reshape(4, 32, D)[:, ::-1]
        k_sample[0, sl] = np.asarray(r["k_s"]).reshape(4, 32, 8, 64)[:, ::-1]
        v_sample[0, sl] = np.asarray(r["v_s"]).reshape(4, 32, 8, 64)[:, ::-1]
        cs = np.asarray(r["conv_s"])
        u = np.transpose(cs, (2, 1, 0, 3)).reshape(4, 512, 2)
        conv_sample[0, sl, 0] = u[:, :, 1]
        conv_sample[0, sl, 1] = u[:, :, 0]
    return (y_prompt, y_sample, k_prompt, v_prompt, conv_prompt, k_sample, v_sample, conv_sample)
```
